# Optimizing a Trainium2 kernel written in Bass

```python
import math
import jax, jax.numpy as jnp
from jax import lax
import numpy as np

D_MODEL = 2048
BATCH = 8
SEQ = 4096
DEPTH = 2

D_MIX = D_MODEL
D_A = D_MIX // 2
D_B = D_MIX - D_A
A_GROUPS = 16
B_HEADS = 16
B_HEAD_DIM = D_B // B_HEADS
CONV_A_WIDTH = 3
CONV_B_WIDTH = 4
LRU_C = 8.0
D_IN_EVEN = 3 * D_A + 2 * D_B
SB_HEADS = 16
SB_HEAD_DIM = D_MODEL // SB_HEADS
Q_BLOCK = 128
D_FF = 4 * D_MODEL
NORM_EPS = 1e-6
N_EVEN = (DEPTH + 1) // 2
N_ODD = DEPTH // 2

kernel_name = "hybrid_conv_rglru_stickbreak_block"


def rms_norm(x, g):
    xf = x.astype(jnp.float32)
    y = xf * lax.rsqrt(jnp.mean(xf * xf, axis=-1, keepdims=True) + NORM_EPS)
    return (y * g.astype(jnp.float32)).astype(x.dtype)


def causal_dwconv(x, w, bias=None):
    k_width = w.shape[0]
    s = x.shape[1]
    xp = jnp.pad(x, ((0, 0), (k_width - 1, 0), (0, 0)))
    y = w[k_width - 1] * x
    for k in range(k_width - 1):
        y = y + w[k] * xp[:, k:k + s]
    if bias is not None:
        y = y + bias
    return y


def rg_lru(x, w_a, b_a, w_x, b_x, lam):
    bsz, s, _ = x.shape
    xf = x.astype(jnp.float32)
    xh = xf.reshape(bsz, s, B_HEADS, B_HEAD_DIM)
    r = jax.nn.sigmoid(jnp.einsum('bshi,hij->bshj', xh, w_a.astype(jnp.float32)).reshape(bsz, s, D_B)
                       + b_a.astype(jnp.float32))
    i = jax.nn.sigmoid(jnp.einsum('bshi,hij->bshj', xh, w_x.astype(jnp.float32)).reshape(bsz, s, D_B)
                       + b_x.astype(jnp.float32))
    log_a = LRU_C * r * jax.nn.log_sigmoid(lam.astype(jnp.float32))
    a = jnp.exp(log_a)
    mult = jnp.sqrt(-jnp.expm1(2.0 * log_a))
    b = mult * (i * xf)

    def combine(e1, e2):
        a1, b1 = e1
        a2, b2 = e2
        return a1 * a2, a2 * b1 + b2

    _, h = lax.associative_scan(combine, (a, b), axis=1)
    return h.astype(x.dtype)


def conv_lru_mixer(h, w_in, conv_a, conv_b, conv_b_bias, rg_w_a, rg_b_a, rg_w_x, rg_b_x, rg_lambda, w_out):
    proj = h @ w_in
    a_bgate, a_cgate, a_x, b_gate, b_x = jnp.split(
        proj, [D_A, 2 * D_A, 3 * D_A, 3 * D_A + D_B], axis=-1)
    y_a = a_bgate * causal_dwconv(a_cgate * a_x, conv_a)
    xr = causal_dwconv(b_x, conv_b, conv_b_bias)
    y_b = rg_lru(xr, rg_w_a, rg_b_a, rg_w_x, rg_b_x, rg_lambda) * jax.nn.gelu(b_gate, approximate=True)
    return jnp.concatenate([y_a, y_b], axis=-1) @ w_out


def stick_breaking_attention(h, w_qkv, w_o):
    bsz, s, _ = h.shape
    qkv = h @ w_qkv
    q, k, v = jnp.split(qkv, 3, axis=-1)
    to_heads = lambda t: t.reshape(bsz, s, SB_HEADS, SB_HEAD_DIM).transpose(0, 2, 1, 3)
    q, k, v = to_heads(q), to_heads(k), to_heads(v)
    n_blocks = s // Q_BLOCK
    q_blocks = q.reshape(bsz, SB_HEADS, n_blocks, Q_BLOCK, SB_HEAD_DIM).transpose(2, 0, 1, 3, 4)
    starts = jnp.arange(n_blocks, dtype=jnp.int32) * Q_BLOCK
    scale = 1.0 / math.sqrt(SB_HEAD_DIM)
    kf = k.astype(jnp.float32)
    vf = v.astype(jnp.float32)
    key_pos = jnp.arange(s, dtype=jnp.int32)[None, :]

    def block(args):
        q_blk, start = args
        z = jnp.einsum('bhqd,bhkd->bhqk', q_blk.astype(jnp.float32), kf) * scale
        q_pos = start + jnp.arange(Q_BLOCK, dtype=jnp.int32)[:, None]
        causal = key_pos < q_pos
        log_not = jnp.where(causal, jax.nn.log_sigmoid(-z), 0.0)
        suffix = lax.cumsum(log_not, axis=3, reverse=True) - log_not
        w = jnp.where(causal, jnp.exp(jax.nn.log_sigmoid(z) + suffix), 0.0)
        return jnp.einsum('bhqk,bhkd->bhqd', w, vf)

    out = lax.map(block, (q_blocks, starts))
    out = out.transpose(1, 0, 3, 2, 4).reshape(bsz, s, D_MODEL).astype(h.dtype)
    return out @ w_o


def sq_relu_mlp(h, w_up, w_down):
    u = jax.nn.relu(h @ w_up)
    return (u * u) @ w_down


def setup_inputs(seed: int = 0) -> dict:
    key = jax.random.key(seed)
    ks = jax.random.split(key, 20)
    nrm = lambda k, shape, fan_in: jax.random.normal(k, shape, jnp.float32) * (fan_in ** -0.5)
    x = jax.random.normal(ks[0], (BATCH, SEQ, D_MODEL), jnp.float32)
    norm_gains = 1.0 + 0.05 * jax.random.normal(ks[1], (DEPTH, 4, D_MODEL), jnp.float32)
    hyb_w_in = nrm(ks[2], (N_EVEN, D_MODEL, D_IN_EVEN), D_MODEL)
    hyb_conv_a = nrm(ks[3], (N_EVEN, CONV_A_WIDTH, D_A), CONV_A_WIDTH)
    hyb_conv_b = nrm(ks[4], (N_EVEN, CONV_B_WIDTH, D_B), CONV_B_WIDTH)
    hyb_conv_b_bias = 0.02 * jax.random.normal(ks[5], (N_EVEN, D_B), jnp.float32)
    hyb_rg_w_a = nrm(ks[6], (N_EVEN, B_HEADS, B_HEAD_DIM, B_HEAD_DIM), B_HEAD_DIM)
    hyb_rg_b_a = 0.02 * jax.random.normal(ks[7], (N_EVEN, D_B), jnp.float32)
    hyb_rg_w_x = nrm(ks[8], (N_EVEN, B_HEADS, B_HEAD_DIM, B_HEAD_DIM), B_HEAD_DIM)
    hyb_rg_b_x = 0.02 * jax.random.normal(ks[9], (N_EVEN, D_B), jnp.float32)
    u = jax.random.uniform(ks[10], (N_EVEN, D_B), jnp.float32, 0.9, 0.999)
    sig = u ** (1.0 / LRU_C)
    hyb_rg_lambda = jnp.log(sig) - jnp.log1p(-sig)
    hyb_w_out = nrm(ks[11], (N_EVEN, D_MIX, D_MODEL), D_MIX)
    sb_w_qkv = nrm(ks[12], (N_ODD, D_MODEL, 3 * D_MODEL), D_MODEL)
    sb_w_o = nrm(ks[13], (N_ODD, D_MODEL, D_MODEL), D_MODEL)
    mlp_w_up = nrm(ks[14], (DEPTH, D_MODEL, D_FF), D_MODEL)
    mlp_w_down = nrm(ks[15], (DEPTH, D_FF, D_MODEL), D_FF)
    return {"x": x, "norm_gains": norm_gains, "hyb_w_in": hyb_w_in, "hyb_conv_a": hyb_conv_a,
            "hyb_conv_b": hyb_conv_b, "hyb_conv_b_bias": hyb_conv_b_bias, "hyb_rg_w_a": hyb_rg_w_a,
            "hyb_rg_b_a": hyb_rg_b_a, "hyb_rg_w_x": hyb_rg_w_x, "hyb_rg_b_x": hyb_rg_b_x,
            "hyb_rg_lambda": hyb_rg_lambda, "hyb_w_out": hyb_w_out, "sb_w_qkv": sb_w_qkv,
            "sb_w_o": sb_w_o, "mlp_w_up": mlp_w_up, "mlp_w_down": mlp_w_down}


def reference(x, norm_gains, hyb_w_in, hyb_conv_a, hyb_conv_b, hyb_conv_b_bias, hyb_rg_w_a, hyb_rg_b_a,
              hyb_rg_w_x, hyb_rg_b_x, hyb_rg_lambda, hyb_w_out, sb_w_qkv, sb_w_o, mlp_w_up, mlp_w_down):
    for layer in range(DEPTH):
        g = norm_gains[layer]
        h = rms_norm(x, g[0])
        if layer % 2 == 0:
            e = layer // 2
            mix = conv_lru_mixer(h, hyb_w_in[e], hyb_conv_a[e], hyb_conv_b[e], hyb_conv_b_bias[e],
                                 hyb_rg_w_a[e], hyb_rg_b_a[e], hyb_rg_w_x[e], hyb_rg_b_x[e],
                                 hyb_rg_lambda[e], hyb_w_out[e])
        else:
            o = layer // 2
            mix = stick_breaking_attention(h, sb_w_qkv[o], sb_w_o[o])
        x = x + rms_norm(mix, g[1])
        h = rms_norm(x, g[2])
        x = x + rms_norm(sq_relu_mlp(h, mlp_w_up[layer], mlp_w_down[layer]), g[3])
    return x
```

```python
import contextlib
import math
import numpy as np
import concourse.bass as bass
import concourse.mybir as mybir
from concourse.bass_utils import run_bass_kernel_spmd

F32 = mybir.dt.float32
BF16 = mybir.dt.bfloat16
AF = mybir.ActivationFunctionType
ALU = mybir.AluOpType

EPOCH = 30000
S_LEN = 4096
D = 2048
T = 512
NT = S_LEN // T
NKC = D // 128
EPS = 1e-6


class Op:
    __slots__ = ("idx", "eng", "emit", "deps", "dma", "dma_val", "signal", "sig")


class Sched:
    def __init__(self):
        self.ops = []
        self.last_w = {}
        self.readers = {}
        self.dma_cnt = {}
        self.dma_last = {}
        self.last_compute = {}
        self.phase_op = None

    def op(self, eng, emit, reads=(), writes=(), dma=None, barrier=False):
        o = Op()
        o.idx = len(self.ops)
        o.eng = eng
        o.emit = emit
        o.dma = dma
        o.signal = False
        o.sig = None
        deps = set()
        if barrier:
            deps.update(self.last_compute.values())
            deps.update(self.dma_last.values())
        elif self.phase_op is not None:
            deps.add(self.phase_op)
        for r in reads:
            w = self.last_w.get(r)
            if w is not None:
                deps.add(w)
        for r in writes:
            w = self.last_w.get(r)
            if w is not None:
                deps.add(w)
            for rd in self.readers.get(r, {}).values():
                deps.add(rd)
        dd = {}
        for d in deps:
            od = self.ops[d]
            if od.dma is not None:
                dd[d] = 16 * self.dma_cnt[od.dma]
            else:
                if od.eng == "pe" and eng == "pe" and dma is None:
                    continue
                dd[d] = None
        o.deps = dd
        rkey = eng if dma is None else ("dma", o.idx)
        for r in reads:
            self.readers.setdefault(r, {})[rkey] = o.idx
        for r in writes:
            self.last_w[r] = o.idx
            self.readers[r] = {}
        if dma is not None:
            c = self.dma_cnt.get(dma, 0) + 1
            self.dma_cnt[dma] = c
            o.dma_val = 16 * c
            self.dma_last[dma] = o.idx
        else:
            self.last_compute[eng] = o.idx
        if barrier:
            self.phase_op = o.idx
        self.ops.append(o)
        return o

    def barrier(self):
        self.op("dve", lambda e: e.memset(self.bar_t[:], 0.0), barrier=True)

    def finalize(self):
        for o in self.ops:
            for d, v in o.deps.items():
                if v is None:
                    self.ops[d].signal = True
        cnt = {}
        for o in self.ops:
            if o.signal:
                k = cnt.get(o.eng, 0)
                cnt[o.eng] = k + 1
                o.sig = (o.eng, k // EPOCH, k % EPOCH + 1)
        self.n_epochs = {e: (c + EPOCH - 1) // EPOCH for e, c in cnt.items()}
        return cnt

    def emit_all(self, nc, stack):
        cnt = self.finalize()
        sems = {}
        for e, n in self.n_epochs.items():
            for ep in range(n):
                sems[("eng", e, ep)] = stack.enter_context(nc.semaphore(f"s_{e}_{ep}"))
        for k in self.dma_cnt:
            sems[("dma", k)] = stack.enter_context(nc.semaphore(f"d_{len(sems)}"))
        streams = {}
        for o in self.ops:
            streams.setdefault(o.eng, []).append(o)
        ops = self.ops

        def run_stream(engobj, lst):
            waited = {}
            for o in lst:
                need = {}
                for d, v in o.deps.items():
                    od = ops[d]
                    if od.dma is not None:
                        key = ("dma", od.dma)
                        val = v
                    else:
                        key = ("eng", od.sig[0], od.sig[1])
                        val = od.sig[2]
                    if waited.get(key, 0) >= val:
                        continue
                    if need.get(key, 0) < val:
                        need[key] = val
                for key, val in need.items():
                    engobj.wait_ge(sems[key], val)
                    waited[key] = val
                ins = o.emit(engobj)
                if o.dma is not None:
                    ins.then_inc(sems[("dma", o.dma)], 16)
                elif o.signal:
                    ins.then_inc(sems[("eng", o.sig[0], o.sig[1])], 1)

        block = stack.enter_context(nc.Block())
        if "sp" in streams:
            @block.sync
            def _(e):
                run_stream(e, streams["sp"])
        if "pe" in streams:
            @block.tensor
            def _(e):
                run_stream(e, streams["pe"])
        if "act" in streams:
            @block.scalar
            def _(e):
                run_stream(e, streams["act"])
        if "dve" in streams:
            @block.vector
            def _(e):
                run_stream(e, streams["dve"])
        if "pool" in streams:
            @block.gpsimd
            def _(e):
                run_stream(e, streams["pool"])
        return cnt


class Ring:
    def __init__(self, name, n):
        self.name = name
        self.n = n
        self.i = 0

    def next(self):
        s = self.i % self.n
        self.i += 1
        return s, (self.name, s)


WSPEC = {
    "win": (40 * 128, 2048),
    "wout": (4 * 2 * 128, 4096),
    "wup0": (64 * 128, 2048),
    "wdn0": (4 * 8 * 128, 4096),
    "wqk": (32 * 128, 2048),
    "wv": (4 * 2 * 128, 4096),
    "wo": (4 * 2 * 128, 4096),
    "wup1": (64 * 128, 2048),
    "wdn1": (4 * 8 * 128, 4096),
}
NVEC = 24 + 32 + 8 * 4
V_CA, V_CB, V_CBB, V_BA, V_BX, V_LAM = 0, 24, 56, 64, 72, 80
ALL_PHASES = ("p1", "p2", "p3a", "p3b", "p3c", "p4")


def build(phases=ALL_PHASES, dbg=()):
    nc = bass.Bass("TRN2", target_bir_lowering=False)
    S = Sched()

    def dram(name, shape, dt, kind=None):
        if kind is None and name in dbg:
            kind = "ExternalOutput"
        if kind is None:
            return nc.dram_tensor(name, shape, dt).ap()
        return nc.dram_tensor(name, shape, dt, kind=kind).ap()

    x_d = dram("x", [S_LEN, D], F32, "ExternalInput")
    gbc_d = dram("gbc", [8 * 128, D], F32, "ExternalInput")
    vec_d = dram("vec", [128, NVEC], F32, "ExternalInput")
    cst_d = dram("cst", [128, 5 * 128], F32, "ExternalInput")
    wbd_d = dram("wbd", [128, 16 * 128], F32, "ExternalInput")
    wf = {k: dram(k + "_f", list(v), F32, "ExternalInput") for k, v in WSPEC.items()}
    wb = {k: dram(k + "_b", list(v), BF16) for k, v in WSPEC.items()}
    y_d = dram("y", [S_LEN, D], F32, "ExternalOutput")
    xA = dram("xA", [S_LEN, D], F32)
    xB = dram("xB", [S_LEN, D], F32)
    xC = dram("xC", [S_LEN, D], F32)
    qT_s = dram("qT_s", [16 * 128, S_LEN], BF16)
    kT_s = dram("kT_s", [16 * 128, S_LEN], BF16)
    v_s = dram("v_s", [S_LEN, D], BF16)
    aT_s = dram("aT_s", [D, S_LEN], BF16)

    with contextlib.ExitStack() as top:
        _uid = [0]

        def sbt(st, n, sh, dt):
            _uid[0] += 1
            return st.enter_context(nc.sbuf_tensor("sb%d_%s" % (_uid[0], n), sh, dt))
        PB = [top.enter_context(nc.psum_tensor(f"pb{i}", [128, 512], F32)) for i in range(6)]
        TP = [top.enter_context(nc.psum_tensor(f"tp{i}", [128, 1024], BF16)) for i in range(2)]
        bankring = Ring("ps", 6)
        cstf = sbt(top, "cstf", [128, 5 * 128], F32)
        cstb = sbt(top, "cstb", [128, 5 * 128], BF16)
        vec = sbt(top, "vec", [128, NVEC], F32)
        epst = sbt(top, "epst", [128, 1], F32)
        onet = sbt(top, "onet", [128, 1], F32)
        S.bar_t = sbt(top, "bar_t", [128, 1], F32)
        ident = cstb[:, 0:128]
        ntri = cstb[:, 128:256]
        nones = cstb[:, 256:384]
        negm = cstb[:, 384:512]
        zeros = cstb[:, 512:640]

        S.op("sp", lambda e: e.dma_start(out=cstf[:], in_=cst_d), writes=["cstf"], dma="c0")
        S.op("sp", lambda e: e.dma_start(out=vec[:], in_=vec_d), writes=["vec"], dma="c1")
        S.op("dve", lambda e: e.tensor_copy(out=cstb[:], in_=cstf[:]), reads=["cstf"], writes=["cstb"])
        S.op("dve", lambda e: e.memset(epst[:], EPS), writes=["epst"])
        S.op("dve", lambda e: e.memset(onet[:], 1.0), writes=["onet"])
        cast_jobs = []

        def add_cast(k):
            rows, cols = WSPEC[k]
            step = (8 << 20) // (cols * 4)
            for r0 in range(0, rows, step):
                cast_jobs.append((k, r0, min(rows, r0 + step), r0 // step))

        def issue_casts(n):
            for _ in range(n):
                if not cast_jobs:
                    return
                k, r0, r1, ci = cast_jobs.pop(0)
                sem = "cast_%s_%d" % (k, ci) if k in ("win", "wout") else "cast_" + k
                S.op("pool", lambda e, k=k, r0=r0, r1=r1: e.dma_start(out=wb[k][r0:r1, :], in_=wf[k][r0:r1, :]),
                     writes=[("wb", k, ci)], dma=sem)

        for ph, ws in (("p1", ["win", "wout"]), ("p2", ["wup0", "wdn0"]), ("p3a", ["wqk", "wv"]), ("p3c", ["wo"]), ("p4", ["wup1", "wdn1"])):
            if ph in phases:
                for k in ws:
                    add_cast(k)
        issue_casts(7 if "p1" in phases else 4)

        def wres_rows(k, r0, r1):
            step = (8 << 20) // (WSPEC[k][1] * 4)
            return [("wb", k, i) for i in range(r0 // step, (r1 - 1) // step + 1)]

        def prep_tile(st_bufs, xsrc, t, gidx, hname="hT", part="all", ctx=None, tbs=(0, 1, 2, 3)):
            xin, xring, hbf, hring, hT, gbc, small = st_bufs
            if ctx is None:
                ctx = {}
            if part in ("all", "A"):
                S.op("dve", lambda e: e.memset(small[:, tbs[0]:tbs[-1] + 1], 0.0), writes=[("ssA", i) for i in tbs])
                if len(tbs) <= 2:
                    for tb in tbs:
                        xs, xres = xring.next()
                        ctx[("x", tb)] = (xs, xres)
                        r0 = t * T + tb * 128
                        S.op("act", lambda e, xs=xs, r0=r0: e.dma_start(out=xin[:, xs, :], in_=xsrc[r0:r0 + 128, :]),
                             reads=[("xd", id(xsrc), r0 // 128)], writes=[xres], dma="xin%d" % xs)
            for tb in tbs:
                if part in ("all", "A"):
                    hs, hres = hring.next()
                    ctx[tb] = (hs, hres)
                    r0 = t * T + tb * 128
                    if ("x", tb) in ctx:
                        xs, xres = ctx[("x", tb)]
                    else:
                        xs, xres = xring.next()
                        S.op("act", lambda e, xs=xs, r0=r0: e.dma_start(out=xin[:, xs, :], in_=xsrc[r0:r0 + 128, :]),
                             reads=[("xd", id(xsrc), r0 // 128)], writes=[xres], dma="xin%d" % xs)
                    ss = small[:, tb:tb + 1]
                    sd = small[:, 4 + tb:5 + tb]
                    rs = small[:, 8 + tb:9 + tb]
                    S.op("act", lambda e, xs=xs, hs=hs, ss=ss: e.activation(out=hbf[:, hs, :], in_=xin[:, xs, :], func=AF.Square, accum_out=ss),
                         reads=[xres, ("ssA", tb)], writes=[hres, ("ssA", tb)])
                    S.op("act", lambda e, ss=ss, sd=sd: e.activation(out=sd, in_=ss, func=AF.Sqrt, bias=epst[:], scale=1.0 / D),
                         reads=[("ssA", tb), "epst"], writes=[("sdA", tb)])
                    S.op("dve", lambda e, sd=sd, rs=rs: e.reciprocal(out=rs, in_=sd), reads=[("sdA", tb)], writes=[("rsA", tb)])
                    S.op("dve", lambda e, xs=xs, hs=hs, rs=rs: e.scalar_tensor_tensor(out=hbf[:, hs, :], in0=xin[:, xs, :], scalar=rs, in1=gbc[:],
                                                                                     op0=ALU.mult, op1=ALU.mult),
                         reads=[xres, ("rsA", tb), "gpre"], writes=[hres])
                if part in ("all", "B"):
                    hs, hres = ctx[tb]
                    for half in range(2):
                        for k in range(8):
                            kc = half * 8 + k
                            S.op("pe", lambda e, half=half, k=k, kc=kc, hs=hs: e.transpose(out=TP[half][:, k * 128:(k + 1) * 128],
                                                                                           in_=hbf[:, hs, kc * 128:(kc + 1) * 128], identity=ident),
                                 reads=[hres, "cstb"], writes=[("tp", half)])
                        dst = hT[:, half * 8:(half + 1) * 8, tb * 128:(tb + 1) * 128]
                        src = TP[half][:].rearrange("p (k t) -> p k t", k=8)
                        if half == 0:
                            S.op("act", lambda e, dst=dst, src=src: e.activation(out=dst, in_=src, func=AF.Copy),
                                 reads=[("tp", half)], writes=[(hname, tb)])
                        else:
                            S.op("dve", lambda e, dst=dst, src=src: e.tensor_copy(out=dst, in_=src),
                                 reads=[("tp", half)], writes=[(hname, tb)])
            return ctx

        def gemm_feat(rhs_of, rhs_res, nkc, wname, nunits, G, wbuf, wring, evac):
            for u0 in range(0, nunits, G):
                ws, wr = wring.next()
                S.op("sp", lambda e, ws=ws, u0=u0: e.dma_start(out=wbuf[:, ws, :, :],
                                                               in_=wb[wname][u0 * 128:(u0 + G) * 128, :].rearrange("(g p) l -> p g l", p=128)),
                     reads=wres_rows(wname, u0 * 128, (u0 + G) * 128), writes=[wr], dma="wf%d" % ws)
                for g in range(G):
                    b, br = bankring.next()
                    for kc in range(nkc):
                        S.op("pe", lambda e, b=b, ws=ws, g=g, kc=kc: e.matmul(PB[b][:], lhsT=wbuf[:, ws, g, kc * 128:(kc + 1) * 128],
                                                                               rhs=rhs_of(kc), start=(kc == 0), stop=(kc == nkc - 1)),
                             reads=[wr] + rhs_res, writes=[br])
                    evac(u0 + g, b, br)

        def gemm_tok(lhs_of, lhs_res, nkc, wname, wbuf, wring, evac, ks_order=None):
            nks = nkc // 8
            for nb in range(4):
                banks = [bankring.next() for _ in range(4)]
                korder = list(ks_order) if ks_order is not None else list(range(nks))
                for ki, ks in enumerate(korder):
                    ws, wr = wring.next()
                    blk = nb * nks + ks
                    S.op("sp", lambda e, ws=ws, blk=blk: e.dma_start(out=wbuf[:, ws, :], in_=wb[wname][blk * 128:(blk + 1) * 128, :]),
                         reads=wres_rows(wname, blk * 128, (blk + 1) * 128), writes=[wr], dma="wt%d" % ws)
                    for tb in range(4):
                        b, br = banks[tb]
                        for k in range(8):
                            kc = ks * 8 + k
                            S.op("pe", lambda e, b=b, ws=ws, k=k, kc=kc, tb=tb, ki=ki: e.matmul(PB[b][:], lhsT=lhs_of(kc)[:, tb * 128:(tb + 1) * 128],
                                                                                         rhs=wbuf[:, ws, k * 512:(k + 1) * 512],
                                                                                         start=(ki == 0 and k == 0), stop=(ki == nks - 1 and k == 7)),
                                 reads=[wr] + lhs_res, writes=[br])
                for tb in range(4):
                    b, br = banks[tb]
                    evac(tb, nb, b, br)

        def post_init(xsrc, xdst, t):
            r0 = t * T
            S.op("pool", lambda e: e.dma_start(out=xdst[r0:r0 + T, :], in_=xsrc[r0:r0 + T, :]),
                 reads=[("xd", id(xsrc), r0 // 128 + i) for i in range(4)], writes=[("xd", id(xdst), r0 // 128 + i) for i in range(4)], dma="xcp")

        def post_tile(mix, xin, xring, gbc, small, junk, xsrc, xdst, t, gidx, final=False):
            S.op("dve", lambda e: e.memset(small[:, 12:16], 0.0), writes=[("ssB", i) for i in range(4)])
            for tb in range(4):
                r0 = t * T + tb * 128
                ss = small[:, 12 + tb:13 + tb]
                sd = small[:, 16 + tb:17 + tb]
                rs = small[:, 20 + tb:21 + tb]
                mres = ("mix", tb)
                S.op("act", lambda e, tb=tb, ss=ss: e.activation(out=junk[:], in_=mix[:, tb, :], func=AF.Square, accum_out=ss),
                     reads=[mres, ("ssB", tb)], writes=["junk", ("ssB", tb)])
                S.op("act", lambda e, ss=ss, sd=sd: e.activation(out=sd, in_=ss, func=AF.Sqrt, bias=epst[:], scale=1.0 / D),
                     reads=[("ssB", tb), "epst"], writes=[("sdB", tb)])
                S.op("dve", lambda e, sd=sd, rs=rs: e.reciprocal(out=rs, in_=sd), reads=[("sdB", tb)], writes=[("rsB", tb)])
                S.op("dve", lambda e, tb=tb, rs=rs: e.scalar_tensor_tensor(out=mix[:, tb, :], in0=mix[:, tb, :], scalar=rs, in1=gbc[:],
                                                                           op0=ALU.mult, op1=ALU.mult),
                     reads=[mres, ("rsB", tb), "gpost"], writes=[mres])
                S.op("pool", lambda e, tb=tb, r0=r0: e.dma_start(out=xdst[r0:r0 + 128, :], in_=mix[:, tb, :], accum_op=ALU.add),
                     reads=[mres], writes=[("xd", id(xdst), r0 // 128)], dma="xout%d" % tb)

        def load_gains(gpre, gi_pre, gpost, gi_post):
            if gpre is not None:
                S.op("sp", lambda e: e.dma_start(out=gpre[:], in_=gbc_d[gi_pre * 128:(gi_pre + 1) * 128, :]), writes=["gpre"], dma="gbc")
            if gpost is not None:
                S.op("sp", lambda e: e.dma_start(out=gpost[:], in_=gbc_d[gi_post * 128:(gi_post + 1) * 128, :]), writes=["gpost"], dma="gbc")

        def phase_p1():
            with contextlib.ExitStack() as st:
                xin = sbt(st, "xin", [128, 2, D], F32)
                hbf = sbt(st, "hbf", [128, 4, D], BF16)
                hT = sbt(st, "hT", [128, NKC, T], BF16)
                yT = sbt(st, "yT", [128, NKC, T], BF16)
                mix = sbt(st, "mix", [128, 4, D], F32)
                gbc = sbt(st, "gpre", [128, D], F32)
                gpost = sbt(st, "gpost", [128, D], F32)
                small = sbt(st, "small", [128, 32], F32)
                wfb = sbt(st, "wfb", [128, 2, 3, D], BF16)
                wtb = sbt(st, "wtb", [128, 2, 4096], BF16)
                uev = sbt(st, "uev", [128, 4, T], F32)
                cx = sbt(st, "cx", [128, 1, T + 2], F32)
                ycv = sbt(st, "ycv", [128, 1, T], F32)
                bxh = sbt(st, "bxh", [128, 1, T + 3], F32)
                xr = sbt(st, "xr", [128, 2, T], F32)
                xrb = sbt(st, "xrb", [128, 2, T], BF16)
                tr_ = sbt(st, "tr_", [128, 1, T], F32)
                ti_ = sbt(st, "ti_", [128, 1, T], F32)
                ta_ = sbt(st, "ta_", [128, 1, T], F32)
                tm_ = sbt(st, "tm_", [128, 1, T], F32)
                tb_ = sbt(st, "tb_", [128, 1, T], F32)
                hs_ = sbt(st, "hs_", [128, 1, T], F32)
                gl_ = sbt(st, "gl_", [128, 1, T], F32)
                gu_ = sbt(st, "gu_", [128, 1, T], F32)
                haloA = sbt(st, "haloA", [128, 8, 2], F32)
                haloB = sbt(st, "haloB", [128, 8, 3], F32)
                hstate = sbt(st, "hstate", [128, 8], F32)
                cl = sbt(st, "cl", [128, 8], F32)
                wbdb = sbt(st, "wbdb", [128, 16 * 128], BF16)
                junkp = sbt(st, "junkp", [128, D], BF16)
                xring, hring = Ring("xin", 2), Ring("hbf", 4)
                wfring, wtring, uring = Ring("wfb", 2), Ring("wtb", 2), Ring("uev", 4)
                r2 = {n: Ring(n, 2) for n in ("cx", "ycv", "bxh", "xr", "xrb", "tr", "ti", "ta", "tm", "tb", "hs", "gl", "gu")}
                for n in ("gu", "tm", "tb", "tr", "ti", "gl", "ta", "cx", "ycv", "bxh", "hs"):
                    r2[n] = Ring(n, 1)

                S.op("pool", lambda e: e.dma_start(out=wbdb[:], in_=wbd_d), writes=["wbdb"], dma="c2")
                S.op("dve", lambda e: e.memset(haloA[:], 0.0), writes=["haloA"])
                S.op("dve", lambda e: e.memset(haloB[:], 0.0), writes=["haloB"])
                S.op("dve", lambda e: e.memset(hstate[:], 0.0), writes=["hstate"])
                S.op("act", lambda e: e.activation(out=cl[:], in_=vec[:, V_LAM:V_LAM + 8], func=AF.Exp, scale=-1.0), reads=["vec"], writes=["cl"])
                S.op("act", lambda e: e.activation(out=cl[:], in_=cl[:], func=AF.Ln, bias=onet[:], scale=1.0), reads=["cl", "onet"], writes=["cl"])
                S.op("dve", lambda e: e.tensor_scalar(out=cl[:], in0=cl[:], scalar1=-8.0, scalar2=None, op0=ALU.mult), reads=["cl"], writes=["cl"])

                bufs = (xin, xring, hbf, hring, hT, gbc, small)
                load_gains(gbc, 0, gpost, 1)
                prep_tile(bufs, x_d, 0, 0)
                for t in range(NT):
                    issue_casts(2)
                    post_init(x_d, xA, t)
                    ev = {}
                    bctx = {}

                    def evac_in(u, b, br):
                        c, r = divmod(u, 5)
                        kind = (3, 4, 0, 1, 2)[r]
                        if kind == 4:
                            s, res = r2["bxh"].next()
                            S.op("pool", lambda e, s=s, c=c: e.tensor_copy(out=bxh[:, s, 0:3], in_=haloB[:, c, :]), reads=["haloB"], writes=[res])
                            S.op("act", lambda e, s=s, b=b: e.activation(out=bxh[:, s, 3:T + 3], in_=PB[b][:], func=AF.Copy), reads=[br], writes=[res])
                            S.op("pool", lambda e, s=s, c=c: e.tensor_copy(out=haloB[:, c, :], in_=bxh[:, s, T:T + 3]), reads=[res], writes=["haloB"])
                            ev[kind] = (bxh[:, s, :], res)
                        else:
                            s, res = uring.next()
                            S.op("act", lambda e, s=s, b=b: e.activation(out=uev[:, s, :], in_=PB[b][:], func=AF.Copy), reads=[br], writes=[res])
                            ev[kind] = (uev[:, s, :], res)
                        if kind == 2:
                            mixer_a(c)
                            mixer_b2(c)
                        if kind == 4:
                            mixer_b1(c)

                    def mixer_a(c):
                        (bg, bgr), (cg, cgr), (ax, axr) = ev[0], ev[1], ev[2]
                        s, cres = r2["cx"].next()
                        ys, yres = r2["ycv"].next()
                        w = lambda k: vec[:, V_CA + c * 3 + k:V_CA + c * 3 + k + 1]
                        S.op("pool", lambda e: e.tensor_copy(out=cx[:, s, 0:2], in_=haloA[:, c, :]), reads=["haloA"], writes=[cres])
                        S.op("dve", lambda e: e.tensor_tensor(out=cx[:, s, 2:T + 2], in0=cg, in1=ax, op=ALU.mult), reads=[cgr, axr], writes=[cres])
                        S.op("pool", lambda e: e.tensor_copy(out=haloA[:, c, :], in_=cx[:, s, T:T + 2]), reads=[cres], writes=["haloA"])
                        S.op("dve", lambda e: e.tensor_scalar(out=ycv[:, ys, :], in0=cx[:, s, 2:T + 2], scalar1=w(2), scalar2=None, op0=ALU.mult),
                             reads=[cres, "vec"], writes=[yres])
                        S.op("dve", lambda e: e.scalar_tensor_tensor(out=ycv[:, ys, :], in0=cx[:, s, 1:T + 1], scalar=w(1), in1=ycv[:, ys, :],
                                                                     op0=ALU.mult, op1=ALU.add), reads=[cres, yres], writes=[yres])
                        S.op("dve", lambda e: e.scalar_tensor_tensor(out=ycv[:, ys, :], in0=cx[:, s, 0:T], scalar=w(0), in1=ycv[:, ys, :],
                                                                     op0=ALU.mult, op1=ALU.add), reads=[cres, yres], writes=[yres])
                        S.op("pool", lambda e: e.tensor_tensor(out=yT[:, c, :], in0=bg, in1=ycv[:, ys, :], op=ALU.mult),
                             reads=[bgr, yres], writes=[("yT", c)])

                    def mixer_b1(c):
                        (g, gr), (bx, bxr) = ev[3], ev[4]
                        nx = {n: r2[n].next() for n in ("xr", "xrb")}
                        XR, XRB = xr[:, nx["xr"][0], :], xrb[:, nx["xrb"][0], :]
                        rr = {n: nx[n][1] for n in nx}
                        w = lambda k: vec[:, V_CB + c * 4 + k:V_CB + c * 4 + k + 1]
                        col = lambda base: vec[:, base + c:base + c + 1]
                        S.op("dve", lambda e: e.tensor_scalar(out=XR, in0=bx[:, 3:T + 3], scalar1=w(3), scalar2=col(V_CBB), op0=ALU.mult, op1=ALU.add),
                             reads=[bxr, "vec"], writes=[rr["xr"]])
                        for k in (2, 1, 0):
                            S.op("dve", lambda e, k=k: e.scalar_tensor_tensor(out=XR, in0=bx[:, k:T + k], scalar=w(k), in1=XR, op0=ALU.mult, op1=ALU.add),
                                 reads=[bxr, rr["xr"]], writes=[rr["xr"]])
                        S.op("pool", lambda e: e.tensor_copy(out=XRB, in_=XR), reads=[rr["xr"]], writes=[rr["xrb"]])
                        bctx[c] = (g, gr, nx)

                    def mixer_b2(c):
                        g, gr, nx0 = bctx[c]
                        nx = {n: r2[n].next() for n in ("tr", "ti", "ta", "tm", "tb", "hs", "gl", "gu")}
                        nx.update(nx0)
                        sl = lambda buf, n: buf[:, nx[n][0], :]
                        w = lambda k: vec[:, V_CB + c * 4 + k:V_CB + c * 4 + k + 1]
                        col = lambda base: vec[:, base + c:base + c + 1]
                        XR, XRB, R, I, A, M, Bv, H, GL, GU = (sl(xr, "xr"), sl(xrb, "xrb"), sl(tr_, "tr"), sl(ti_, "ti"), sl(ta_, "ta"),
                                                              sl(tm_, "tm"), sl(tb_, "tb"), sl(hs_, "hs"), sl(gl_, "gl"), sl(gu_, "gu"))
                        rr = {n: nx[n][1] for n in nx}
                        b1, br1 = bankring.next()
                        b2, br2 = bankring.next()
                        S.op("pe", lambda e: e.matmul(PB[b1][:], lhsT=wbdb[:, c * 128:(c + 1) * 128], rhs=XRB, start=True, stop=True),
                             reads=[rr["xrb"], "wbdb"], writes=[br1])
                        S.op("pe", lambda e: e.matmul(PB[b2][:], lhsT=wbdb[:, (8 + c) * 128:(9 + c) * 128], rhs=XRB, start=True, stop=True),
                             reads=[rr["xrb"], "wbdb"], writes=[br2])
                        S.op("act", lambda e: e.activation(out=R, in_=PB[b1][:], func=AF.Sigmoid, bias=col(V_BA), scale=1.0), reads=[br1, "vec"], writes=[rr["tr"]])
                        S.op("act", lambda e: e.activation(out=I, in_=PB[b2][:], func=AF.Sigmoid, bias=col(V_BX), scale=1.0), reads=[br2, "vec"], writes=[rr["ti"]])
                        S.op("pool", lambda e: e.tensor_tensor(out=GU, in0=g, in1=g, op=ALU.mult), reads=[gr], writes=[rr["gu"]])
                        S.op("pool", lambda e: e.tensor_scalar(out=GU, in0=GU, scalar1=0.044715, scalar2=1.0, op0=ALU.mult, op1=ALU.add),
                             reads=[rr["gu"]], writes=[rr["gu"]])
                        S.op("pool", lambda e: e.tensor_tensor(out=GU, in0=GU, in1=g, op=ALU.mult), reads=[rr["gu"], gr], writes=[rr["gu"]])
                        S.op("act", lambda e: e.activation(out=GL, in_=GU, func=AF.Sigmoid, scale=1.5957691216), reads=[rr["gu"]], writes=[rr["gl"]])
                        S.op("pool", lambda e: e.tensor_tensor(out=GL, in0=GL, in1=g, op=ALU.mult), reads=[rr["gl"], gr], writes=[rr["gl"]])
                        S.op("act", lambda e: e.activation(out=A, in_=R, func=AF.Exp, scale=cl[:, c:c + 1]), reads=[rr["tr"], "cl"], writes=[rr["ta"]])
                        S.op("pool", lambda e: e.tensor_tensor(out=M, in0=A, in1=A, op=ALU.mult), reads=[rr["ta"]], writes=[rr["tm"]])
                        S.op("act", lambda e: e.activation(out=M, in_=M, func=AF.Sqrt, bias=onet[:], scale=-1.0), reads=[rr["tm"], "onet"], writes=[rr["tm"]])
                        S.op("dve", lambda e: e.tensor_tensor(out=Bv, in0=I, in1=XR, op=ALU.mult), reads=[rr["ti"], rr["xr"]], writes=[rr["tb"]])
                        S.op("dve", lambda e: e.tensor_tensor(out=Bv, in0=Bv, in1=M, op=ALU.mult), reads=[rr["tb"], rr["tm"]], writes=[rr["tb"]])
                        S.op("dve", lambda e: e.tensor_tensor_scan(out=H, data0=A, data1=Bv, initial=hstate[:, c:c + 1], op0=ALU.mult, op1=ALU.add),
                             reads=[rr["ta"], rr["tb"], "hstate"], writes=[rr["hs"]])
                        S.op("dve", lambda e: e.tensor_copy(out=hstate[:, c:c + 1], in_=H[:, T - 1:T]), reads=[rr["hs"]], writes=["hstate"])
                        S.op("dve", lambda e: e.tensor_tensor(out=yT[:, 8 + c, :], in0=H, in1=GL, op=ALU.mult), reads=[rr["hs"], rr["gl"]], writes=[("yT", 8 + c)])

                    pctx = {}

                    def hook1(t=t, pctx=pctx):
                        if t > 0:
                            post_tile(mix, xin, xring, gpost, small, junkp, x_d, xA, t - 1, 1)
                        if t + 1 < NT:
                            prep_tile(bufs, x_d, t + 1, 0, part="A", ctx=pctx, tbs=(0, 1))

                    def hook2(t=t, pctx=pctx):
                        if t + 1 < NT:
                            prep_tile(bufs, x_d, t + 1, 0, part="A", ctx=pctx, tbs=(2, 3))
                    _gemm_feat_groups(hT, "win", [2, 3] * 8, wfb, wfring, evac_in, mid_hook=[(4, hook1), (10, hook2)])
                    if t + 1 < NT:
                        prep_tile(bufs, x_d, t + 1, 0, part="B", ctx=pctx)

                    def evac_out(tb, nb, b, br):
                        S.op("act", lambda e: e.activation(out=mix[:, tb, nb * 512:(nb + 1) * 512], in_=PB[b][:], func=AF.Copy),
                             reads=[br], writes=[("mix", tb)])
                    gemm_tok(lambda kc: yT[:, kc, :], [("yT", i) for i in range(16)], NKC, "wout", wtb, wtring, evac_out)
                    if t == NT - 1:
                        post_tile(mix, xin, xring, gpost, small, junkp, x_d, xA, t, 1)

        def _gemm_feat_groups(hT, wname, groups, wbuf, wring, evac, nkc=NKC, hname="hT", mid_hook=None):
            u0 = 0
            for gi, G in enumerate(groups):
                if mid_hook is not None:
                    for hk in (mid_hook if isinstance(mid_hook, list) else [mid_hook]):
                        if gi == hk[0]:
                            hk[1]()
                ws, wr = wring.next()
                S.op("sp", lambda e, ws=ws, u0=u0, G=G: e.dma_start(out=wbuf[:, ws, 0:G, :],
                                                                    in_=wb[wname][u0 * 128:(u0 + G) * 128, :].rearrange("(g p) l -> p g l", p=128)),
                     reads=wres_rows(wname, u0 * 128, (u0 + G) * 128), writes=[wr], dma="wf%d" % ws)
                for g in range(G):
                    b, br = bankring.next()
                    for kc in range(nkc):
                        S.op("pe", lambda e, b=b, ws=ws, g=g, kc=kc: e.matmul(PB[b][:], lhsT=wbuf[:, ws, g, kc * 128:(kc + 1) * 128],
                                                                               rhs=hT[:, kc, :], start=(kc == 0), stop=(kc == nkc - 1)),
                             reads=[wr] + [(hname, i) for i in range(4)], writes=[br])
                    evac(u0 + g, b, br)
                u0 += G

        def phase_mlp(layer, xsrc, xdst):
            with contextlib.ExitStack() as st:
                xin = sbt(st, "xin", [128, 2, D], F32)
                hbf = sbt(st, "hbf", [128, 4, D], BF16)
                hT = sbt(st, "hT", [128, NKC, T], BF16)
                uT = sbt(st, "uT", [128, 64, T], BF16)
                mix = sbt(st, "mix", [128, 4, D], F32)
                gbc = sbt(st, "gpre", [128, D], F32)
                gpost = sbt(st, "gpost", [128, D], F32)
                small = sbt(st, "small", [128, 32], F32)
                wfb = sbt(st, "wfb", [128, 2, 2, D], BF16)
                wtb = sbt(st, "wtb", [128, 2, 4096], BF16)
                rl = sbt(st, "rl", [128, 3, T], F32)
                junkp = sbt(st, "junkp", [128, D], BF16)
                xring, hring = Ring("xin", 2), Ring("hbf", 4)
                wfring, wtring, rring = Ring("wfb", 2), Ring("wtb", 2), Ring("rl", 3)
                bufs = (xin, xring, hbf, hring, hT, gbc, small)
                load_gains(gbc, layer * 4 + 2, gpost, layer * 4 + 3)
                prep_tile(bufs, xsrc, 0, layer * 4 + 2)
                for t in range(NT):
                    issue_casts(1 if layer == 0 else 100)
                    post_init(xsrc, xdst, t)

                    def evac_up(u, b, br):
                        s, res = rring.next()
                        S.op("act", lambda e: e.activation(out=rl[:, s, :], in_=PB[b][:], func=AF.Relu), reads=[br], writes=[res])
                        eng = "pool" if u % 2 == 0 else "dve"
                        S.op(eng, lambda e: e.tensor_tensor(out=uT[:, u, :], in0=rl[:, s, :], in1=rl[:, s, :], op=ALU.mult),
                             reads=[res], writes=[("uT", u)])
                    pctx = {}
                    hook = [(10, lambda t=t, pctx=pctx: prep_tile(bufs, xsrc, t + 1, layer * 4 + 2, part="A", ctx=pctx, tbs=(0, 1))),
                            (20, lambda t=t, pctx=pctx: prep_tile(bufs, xsrc, t + 1, layer * 4 + 2, part="A", ctx=pctx, tbs=(2, 3)))] if t + 1 < NT else None
                    _gemm_feat_groups(hT, "wup%d" % layer, [2] * 32, wfb, wfring, evac_up, mid_hook=hook)
                    if t + 1 < NT:
                        prep_tile(bufs, xsrc, t + 1, layer * 4 + 2, part="B", ctx=pctx)

                    def evac_dn(tb, nb, b, br):
                        S.op("act", lambda e: e.activation(out=mix[:, tb, nb * 512:(nb + 1) * 512], in_=PB[b][:], func=AF.Copy),
                             reads=[br], writes=[("mix", tb)])
                    gemm_tok(lambda kc: uT[:, kc, :], [("uT", i) for i in range(64)], 64, "wdn%d" % layer, wtb, wtring, evac_dn)
                    post_tile(mix, xin, xring, gpost, small, junkp, xsrc, xdst, t, layer * 4 + 3)

        def phase_p3a():
            with contextlib.ExitStack() as st:
                xin = sbt(st, "xin", [128, 2, D], F32)
                hbf = sbt(st, "hbf", [128, 4, D], BF16)
                hT2 = sbt(st, "hT2", [128, 2, NKC, T], BF16)
                gbc = sbt(st, "gpre", [128, D], F32)
                small = sbt(st, "small", [128, 32], F32)
                wfb = sbt(st, "wfb", [128, 2, 2, D], BF16)
                wtb = sbt(st, "wtb", [128, 3, 4096], BF16)
                qst = sbt(st, "qst", [128, 4, T], BF16)
                vst = sbt(st, "vst", [128, 4, D], BF16)
                xring, hring = Ring("xin", 2), Ring("hbf", 4)
                wfring, wtring, qring = Ring("wfb", 2), Ring("wtb", 3), Ring("qst", 4)
                qscale = 1.0 / math.sqrt(128.0)
                load_gains(gbc, 4, None, 0)
                mkbufs = lambda t: (xin, xring, hbf, hring, hT2[:, t % 2, :, :], gbc, small)
                prep_tile(mkbufs(0), xB, 0, 4, hname="hT0")
                for t in range(NT):
                    issue_casts(1)
                    hT = hT2[:, t % 2, :, :]
                    hname = "hT%d" % (t % 2)

                    def evac_qk(u, b, br):
                        s, res = qring.next()
                        if u < 16:
                            S.op("act", lambda e: e.activation(out=qst[:, s, :], in_=PB[b][:], func=AF.Copy, scale=qscale), reads=[br], writes=[res])
                            dst = qT_s[u * 128:(u + 1) * 128, t * T:(t + 1) * T]
                            dres = ("qT", u, t)
                        else:
                            S.op("dve", lambda e: e.tensor_copy(out=qst[:, s, :], in_=PB[b][:]), reads=[br], writes=[res])
                            dst = kT_s[(u - 16) * 128:(u - 15) * 128, t * T:(t + 1) * T]
                            dres = ("kT", u - 16, t)
                        S.op("pool", lambda e: e.dma_start(out=dst, in_=qst[:, s, :]), reads=[res], writes=[dres], dma="qst%d" % s)
                    pctx = {}
                    hook = [(5, lambda t=t, pctx=pctx: prep_tile(mkbufs(t + 1), xB, t + 1, 4, hname="hT%d" % ((t + 1) % 2), part="A", ctx=pctx, tbs=(0, 1))),
                            (11, lambda t=t, pctx=pctx: prep_tile(mkbufs(t + 1), xB, t + 1, 4, hname="hT%d" % ((t + 1) % 2), part="A", ctx=pctx, tbs=(2, 3)))] if t + 1 < NT else None
                    _gemm_feat_groups(hT, "wqk", [2] * 16, wfb, wfring, evac_qk, hname=hname, mid_hook=hook)
                    if t + 1 < NT:
                        prep_tile(mkbufs(t + 1), xB, t + 1, 4, hname="hT%d" % ((t + 1) % 2), part="B", ctx=pctx)

                    def evac_v(tb, nb, b, br):
                        S.op("act", lambda e: e.activation(out=vst[:, tb, nb * 512:(nb + 1) * 512], in_=PB[b][:], func=AF.Copy),
                             reads=[br], writes=[("vst", tb)])
                        if nb == 3:
                            r0 = t * T + tb * 128
                            S.op("pool", lambda e: e.dma_start(out=v_s[r0:r0 + 128, :], in_=vst[:, tb, :]),
                                 reads=[("vst", tb)], writes=[("v", r0 // 128)], dma="vst%d" % tb)
                    gemm_tok(lambda kc, hT=hT: hT[:, kc, :], [(hname, i) for i in range(4)], NKC, "wv", wtb, wtring, evac_v)

        def phase_p3b():
            with contextlib.ExitStack() as st:
                qT = sbt(st, "qT", [128, 2, S_LEN], BF16)
                kT = sbt(st, "kT", [128, 2, S_LEN], BF16)
                vv = sbt(st, "vv", [128, 2, 32, 128], BF16)
                et = sbt(st, "et", [128, 2, T], F32)
                spt = sbt(st, "spt", [128, 4, T], BF16)
                acc = sbt(st, "acc", [128, 6, T], BF16)
                wT = sbt(st, "wT", [128, 4, T], BF16)
                ost = sbt(st, "ost", [128, 2, T], BF16)
                hring = Ring("hd", 2)
                ering, sring, wring_, oring = Ring("et", 2), Ring("spt", 4), Ring("wT", 4), Ring("ost", 2)
                aring, bring = Ring("psA", 2), Ring("psB", 2)
                tiles = []
                sbi = 0
                for h in range(16):
                    hsl, hres = hring.next()
                    for I in range(NT):
                        ob = 4 + (sbi % 2)
                        aset = 3 * (sbi % 2)
                        sbi += 1
                        jtop = 4 * I + 3
                        for pos, j in enumerate(range(jtop, -1, -1)):
                            c0 = max(0, j - 4 * I)
                            tiles.append(dict(h=h, hsl=hsl, hres=hres, I=I, j=j, lo=c0 * 128, diag=(j >= 4 * I), first=(j == jtop), last=(j == 0),
                                              ob=ob, a_old=aset + pos % 3, a_new=aset + (pos + 1) % 3, aset=aset, newhead=(I == 0 and j == jtop)))

                def load_head(tl):
                    h, hsl, hres = tl["h"], tl["hsl"], tl["hres"]
                    if h < 8:
                        issue_casts(1)
                    S.op("sp", lambda e: e.dma_start(out=qT[:, hsl, :], in_=qT_s[h * 128:(h + 1) * 128, :]),
                         reads=[("qT", h, t) for t in range(NT)], writes=[hres], dma="hq%d" % hsl)
                    S.op("sp", lambda e: e.dma_start(out=kT[:, hsl, :], in_=kT_s[h * 128:(h + 1) * 128, :]),
                         reads=[("kT", h, t) for t in range(NT)], writes=[hres], dma="hq%d" % hsl)
                    S.op("sp", lambda e: e.dma_start(out=vv[:, hsl, :, :], in_=v_s[:, h * 128:(h + 1) * 128].rearrange("(j p) d -> p j d", p=128)),
                         reads=[("v", i) for i in range(32)], writes=[hres], dma="hq%d" % hsl)

                def stage1(tl):
                    hsl, hres, I, j, lo, diag = tl["hsl"], tl["hres"], tl["I"], tl["j"], tl["lo"], tl["diag"]
                    ob, obr = tl["ob"], ("ps", tl["ob"])
                    cs = slice(lo, T)
                    qs = slice(I * T + lo, (I + 1) * T)
                    ks = slice(j * 128, (j + 1) * 128)
                    if tl["newhead"]:
                        load_head(tl)
                    if tl["first"]:
                        S.op("pe", lambda e: e.matmul(PB[ob][:], lhsT=zeros, rhs=qT[:, hsl, 0:T], start=True, stop=False),
                             reads=["cstb", hres], writes=[obr])
                        for k in range(3):
                            ab = tl["aset"] + k
                            S.op("pool", lambda e, ab=ab: e.memset(acc[:, ab, :], 0.0), writes=[("acc", ab)])
                    ba, bar = aring.next()
                    es, eres = ering.next()
                    ss_, sres = sring.next()
                    tl["ss"], tl["sres"] = ss_, sres
                    S.op("pe", lambda e: e.matmul(PB[ba][:, cs], lhsT=kT[:, hsl, ks], rhs=qT[:, hsl, qs], start=True, stop=not diag),
                         reads=[hres], writes=[("ps", ba)])
                    if diag:
                        S.op("pe", lambda e: e.matmul(PB[ba][:, lo:lo + 128], lhsT=ident, rhs=negm, start=False, stop=True),
                             reads=["cstb"], writes=[("ps", ba)])
                    S.op("act", lambda e: e.activation(out=et[:, es, cs], in_=PB[ba][:, cs], func=AF.Exp), reads=[("ps", ba)], writes=[eres])
                    S.op("act", lambda e: e.activation(out=spt[:, ss_, cs], in_=et[:, es, cs], func=AF.Ln, bias=onet[:], scale=1.0),
                         reads=[eres, "onet"], writes=[sres])
                    if not tl["last"]:
                        a_old, a_new = tl["a_old"], tl["a_new"]
                        S.op("dve", lambda e: e.tensor_tensor(out=acc[:, a_new, cs], in0=acc[:, a_old, cs], in1=spt[:, ss_, cs], op=ALU.add),
                             reads=[("acc", a_old), sres], writes=[("acc", a_new)])

                def stage2(tl):
                    hsl, hres, I, j, lo, diag = tl["hsl"], tl["hres"], tl["I"], tl["j"], tl["lo"], tl["diag"]
                    cs = slice(lo, T)
                    qs = slice(I * T + lo, (I + 1) * T)
                    ks = slice(j * 128, (j + 1) * 128)
                    bb, _ = bring.next()
                    bb += 2
                    bbr = ("ps", bb)
                    ws_, wres_ = wring_.next()
                    tl["ws"], tl["wres"] = ws_, wres_
                    ss_, sres, a_old = tl["ss"], tl["sres"], tl["a_old"]
                    S.op("pe", lambda e: e.matmul(PB[bb][:, cs], lhsT=kT[:, hsl, ks], rhs=qT[:, hsl, qs], start=True, stop=False),
                         reads=[hres], writes=[bbr])
                    if diag:
                        S.op("pe", lambda e: e.matmul(PB[bb][:, lo:lo + 128], lhsT=ident, rhs=negm, start=False, stop=False),
                             reads=["cstb"], writes=[bbr])
                    if not tl["first"]:
                        S.op("pe", lambda e: e.matmul(PB[bb][:, cs], lhsT=nones, rhs=acc[:, a_old, cs], start=False, stop=False),
                             reads=["cstb", ("acc", a_old)], writes=[bbr])
                    S.op("pe", lambda e: e.matmul(PB[bb][:, cs], lhsT=ntri, rhs=spt[:, ss_, cs], start=False, stop=True),
                         reads=["cstb", sres], writes=[bbr])
                    S.op("act", lambda e: e.activation(out=wT[:, ws_, cs], in_=PB[bb][:, cs], func=AF.Exp), reads=[bbr], writes=[wres_])

                def stage3(tl):
                    hsl, hres, I, j, lo, h = tl["hsl"], tl["hres"], tl["I"], tl["j"], tl["lo"], tl["h"]
                    ob, obr = tl["ob"], ("ps", tl["ob"])
                    cs = slice(lo, T)
                    ws_, wres_ = tl["ws"], tl["wres"]
                    S.op("pe", lambda e: e.matmul(PB[ob][:, cs], lhsT=vv[:, hsl, j, :], rhs=wT[:, ws_, cs], start=False, stop=tl["last"]),
                         reads=[hres, wres_], writes=[obr])
                    if tl["last"]:
                        os_, ores = oring.next()
                        S.op("dve", lambda e: e.tensor_copy(out=ost[:, os_, :], in_=PB[ob][:]), reads=[obr], writes=[ores])
                        S.op("pool", lambda e: e.dma_start(out=aT_s[h * 128:(h + 1) * 128, I * T:(I + 1) * T], in_=ost[:, os_, :]),
                             reads=[ores], writes=[("aT", h, I)], dma="ost%d" % os_)

                N = len(tiles)
                for n in range(N + 2):
                    if n < N:
                        stage1(tiles[n])
                    if 0 <= n - 1 < N:
                        stage2(tiles[n - 1])
                    if 0 <= n - 2 < N:
                        stage3(tiles[n - 2])

        def phase_p3c():
            with contextlib.ExitStack() as st:
                xin = sbt(st, "xin", [128, 2, D], F32)
                aT = sbt(st, "aT", [128, 2, NKC, T], BF16)
                mix = sbt(st, "mix", [128, 4, D], F32)
                gbc = sbt(st, "gpost", [128, D], F32)
                junk = sbt(st, "junk", [128, D], BF16)
                small = sbt(st, "small", [128, 32], F32)
                wtb = sbt(st, "wtb", [128, 3, 4096], BF16)
                xring, wtring, aring = Ring("xin", 2), Ring("wtb", 3), Ring("aT", 2)
                issue_casts(100)
                load_gains(None, 0, gbc, 5)
                for t in range(NT):
                    post_init(xB, xC, t)
                    as_, ares = aring.next()
                    S.op("sp", lambda e, as_=as_, t=t: e.dma_start(out=aT[:, as_, :, :],
                                                                   in_=aT_s[:, t * T:(t + 1) * T].rearrange("(kc p) t -> p kc t", p=128)),
                         reads=[("aT", h, t) for h in range(16)], writes=[ares], dma="aT%d" % as_)

                    def evac_o(tb, nb, b, br):
                        S.op("act", lambda e: e.activation(out=mix[:, tb, nb * 512:(nb + 1) * 512], in_=PB[b][:], func=AF.Copy),
                             reads=[br], writes=[("mix", tb)])
                    gemm_tok(lambda kc, as_=as_: aT[:, as_, kc, :], [ares], NKC, "wo", wtb, wtring, evac_o)
                    post_tile(mix, xin, xring, gbc, small, junk, xB, xC, t, 5)

        if "p1" in phases:
            phase_p1()
            S.barrier()
        if "p2" in phases:
            phase_mlp(0, xA, xB)
            S.barrier()
        if "p3a" in phases:
            phase_p3a()
            S.barrier()
        if "p3b" in phases:
            phase_p3b()
            S.barrier()
        if "p3c" in phases:
            phase_p3c()
            S.barrier()
        if "p4" in phases:
            phase_mlp(1, xC, y_d)
        S.op("pool", lambda e: e.nop(), barrier=True)
        cnt = S.emit_all(nc, top)
    return nc, S, cnt


def _feat_units(W, order=None):
    K, N = W.shape
    nu = N // 128
    Wr = W.reshape(K // 128, 128, nu, 128)
    if order is not None:
        Wr = Wr[:, :, order, :]
    return np.ascontiguousarray(Wr.transpose(2, 1, 0, 3)).reshape(nu * 128, (K // 128) * 128)


def _tok_blocks(W):
    K, N = W.shape
    nks = K // 1024
    Wr = W.reshape(nks, 8, 128, N // 512, 512)
    return np.ascontiguousarray(Wr.transpose(3, 0, 2, 1, 4)).reshape((N // 512) * nks * 128, 8 * 512)


def _chunks(v):
    return np.ascontiguousarray(v.reshape(8, 128).T)


def prep_shared(inp):
    f = lambda a: np.ascontiguousarray(np.asarray(a, dtype=np.float32))
    g = f(inp["norm_gains"]).reshape(8, 1, D)
    sh = {}
    sh["gbc"] = np.ascontiguousarray(np.broadcast_to(g, (8, 128, D))).reshape(8 * 128, D)
    ca = f(inp["hyb_conv_a"])[0]
    cb = f(inp["hyb_conv_b"])[0]
    vec = np.zeros((128, NVEC), np.float32)
    vec[:, V_CA:V_CA + 24] = np.stack([_chunks(ca[k]) for k in range(3)], axis=2).reshape(128, 24)
    vec[:, V_CB:V_CB + 32] = np.stack([_chunks(cb[k]) for k in range(4)], axis=2).reshape(128, 32)
    vec[:, V_CBB:V_CBB + 8] = _chunks(f(inp["hyb_conv_b_bias"])[0])
    vec[:, V_BA:V_BA + 8] = _chunks(f(inp["hyb_rg_b_a"])[0])
    vec[:, V_BX:V_BX + 8] = _chunks(f(inp["hyb_rg_b_x"])[0])
    vec[:, V_LAM:V_LAM + 8] = _chunks(f(inp["hyb_rg_lambda"])[0])
    sh["vec"] = vec
    idx = np.arange(128)
    cst = np.zeros((128, 5 * 128), np.float32)
    cst[:, 0:128] = np.eye(128, dtype=np.float32)
    cst[:, 128:256] = -(idx[:, None] >= idx[None, :]).astype(np.float32)
    cst[:, 256:384] = -1.0
    cst[:, 384:512] = np.where(idx[:, None] >= idx[None, :], -30000.0, 0.0)
    sh["cst"] = cst
    wbd = np.zeros((128, 16, 128), np.float32)
    wa = f(inp["hyb_rg_w_a"])[0]
    wx = f(inp["hyb_rg_w_x"])[0]
    for c in range(8):
        for hh in range(2):
            wbd[hh * 64:(hh + 1) * 64, c, hh * 64:(hh + 1) * 64] = wa[2 * c + hh]
            wbd[hh * 64:(hh + 1) * 64, 8 + c, hh * 64:(hh + 1) * 64] = wx[2 * c + hh]
    sh["wbd"] = wbd.reshape(128, 16 * 128)
    order = []
    for c in range(8):
        order += [24 + c, 32 + c, c, 8 + c, 16 + c]
    sh["win_f"] = _feat_units(f(inp["hyb_w_in"])[0], order)
    sh["wout_f"] = _tok_blocks(f(inp["hyb_w_out"])[0])
    wqkv = f(inp["sb_w_qkv"])[0]
    sh["wqk_f"] = _feat_units(wqkv[:, :4096])
    sh["wv_f"] = _tok_blocks(np.ascontiguousarray(wqkv[:, 4096:]))
    sh["wo_f"] = _tok_blocks(f(inp["sb_w_o"])[0])
    for l in range(2):
        sh["wup%d_f" % l] = _feat_units(f(inp["mlp_w_up"])[l])
        sh["wdn%d_f" % l] = _tok_blocks(f(inp["mlp_w_down"])[l])
    return sh


_CACHE = {}


def kernel(**inputs):
    if "nc" not in _CACHE:
        _CACHE["nc"] = build()[0]
    nc = _CACHE["nc"]
    sh = prep_shared(inputs)
    x = np.asarray(inputs["x"], dtype=np.float32)
    in_maps = []
    for c in range(8):
        m = dict(sh)
        m["x"] = np.ascontiguousarray(x[c])
        in_maps.append(m)
    res = run_bass_kernel_spmd(nc, in_maps, core_ids=list(range(8)))
    return np.stack([np.asarray(r["y"], dtype=np.float32) for r in res.results], axis=0)
```

```python
import contextlib
import math
import numpy as np
import concourse.bass as bass
import concourse.mybir as mybir
from concourse.bass_utils import run_bass_kernel_spmd

F32 = mybir.dt.float32
BF16 = mybir.dt.bfloat16
AF = mybir.ActivationFunctionType
ALU = mybir.AluOpType

EPOCH = 30000
S_LEN = 4096
D = 2048
T = 512
NT = S_LEN // T
NKC = D // 128
EPS = 1e-6


class Op:
    __slots__ = ("idx", "eng", "emit", "deps", "dma", "dma_val", "signal", "sig")


class Sched:
    def __init__(self):
        self.ops = []
        self.last_w = {}
        self.readers = {}
        self.dma_cnt = {}
        self.dma_last = {}
        self.last_compute = {}
        self.phase_op = None

    def op(self, eng, emit, reads=(), writes=(), dma=None, barrier=False):
        o = Op()
        o.idx = len(self.ops)
        o.eng = eng
        o.emit = emit
        o.dma = dma
        o.signal = False
        o.sig = None
        deps = set()
        if barrier:
            deps.update(self.last_compute.values())
            deps.update(self.dma_last.values())
        elif self.phase_op is not None:
            deps.add(self.phase_op)
        for r in reads:
            w = self.last_w.get(r)
            if w is not None:
                deps.add(w)
        for r in writes:
            w = self.last_w.get(r)
            if w is not None:
                deps.add(w)
            for rd in self.readers.get(r, {}).values():
                deps.add(rd)
        dd = {}
        for d in deps:
            od = self.ops[d]
            if od.dma is not None:
                dd[d] = 16 * self.dma_cnt[od.dma]
            else:
                if od.eng == "pe" and eng == "pe" and dma is None:
                    continue
                dd[d] = None
        o.deps = dd
        rkey = eng if dma is None else ("dma", o.idx)
        for r in reads:
            self.readers.setdefault(r, {})[rkey] = o.idx
        for r in writes:
            self.last_w[r] = o.idx
            self.readers[r] = {}
        if dma is not None:
            c = self.dma_cnt.get(dma, 0) + 1
            self.dma_cnt[dma] = c
            o.dma_val = 16 * c
            self.dma_last[dma] = o.idx
        else:
            self.last_compute[eng] = o.idx
        if barrier:
            self.phase_op = o.idx
        self.ops.append(o)
        return o

    def barrier(self):
        self.op("dve", lambda e: e.memset(self.bar_t[:], 0.0), barrier=True)

    def finalize(self):
        for o in self.ops:
            for d, v in o.deps.items():
                if v is None:
                    self.ops[d].signal = True
        cnt = {}
        for o in self.ops:
            if o.signal:
                k = cnt.get(o.eng, 0)
                cnt[o.eng] = k + 1
                o.sig = (o.eng, k // EPOCH, k % EPOCH + 1)
        self.n_epochs = {e: (c + EPOCH - 1) // EPOCH for e, c in cnt.items()}
        return cnt

    def emit_all(self, nc, stack):
        cnt = self.finalize()
        sems = {}
        for e, n in self.n_epochs.items():
            for ep in range(n):
                sems[("eng", e, ep)] = stack.enter_context(nc.semaphore(f"s_{e}_{ep}"))
        for k in self.dma_cnt:
            sems[("dma", k)] = stack.enter_context(nc.semaphore(f"d_{len(sems)}"))
        streams = {}
        for o in self.ops:
            streams.setdefault(o.eng, []).append(o)
        ops = self.ops

        def run_stream(engobj, lst):
            waited = {}
            for o in lst:
                need = {}
                for d, v in o.deps.items():
                    od = ops[d]
                    if od.dma is not None:
                        key = ("dma", od.dma)
                        val = v
                    else:
                        key = ("eng", od.sig[0], od.sig[1])
                        val = od.sig[2]
                    if waited.get(key, 0) >= val:
                        continue
                    if need.get(key, 0) < val:
                        need[key] = val
                for key, val in need.items():
                    engobj.wait_ge(sems[key], val)
                    waited[key] = val
                ins = o.emit(engobj)
                if o.dma is not None:
                    ins.then_inc(sems[("dma", o.dma)], 16)
                elif o.signal:
                    ins.then_inc(sems[("eng", o.sig[0], o.sig[1])], 1)

        block = stack.enter_context(nc.Block())
        if "sp" in streams:
            @block.sync
            def _(e):
                run_stream(e, streams["sp"])
        if "pe" in streams:
            @block.tensor
            def _(e):
                run_stream(e, streams["pe"])
        if "act" in streams:
            @block.scalar
            def _(e):
                run_stream(e, streams["act"])
        if "dve" in streams:
            @block.vector
            def _(e):
                run_stream(e, streams["dve"])
        if "pool" in streams:
            @block.gpsimd
            def _(e):
                run_stream(e, streams["pool"])
        return cnt


class Ring:
    def __init__(self, name, n):
        self.name = name
        self.n = n
        self.i = 0

    def next(self):
        s = self.i % self.n
        self.i += 1
        return s, (self.name, s)


WSPEC = {
    "win": (40 * 128, 2048),
    "wout": (4 * 2 * 128, 4096),
    "wup0": (64 * 128, 2048),
    "wdn0": (4 * 8 * 128, 4096),
    "wqk": (32 * 128, 2048),
    "wv": (4 * 2 * 128, 4096),
    "wo": (4 * 2 * 128, 4096),
    "wup1": (64 * 128, 2048),
    "wdn1": (4 * 8 * 128, 4096),
}
NVEC = 24 + 32 + 8 * 4
V_CA, V_CB, V_CBB, V_BA, V_BX, V_LAM = 0, 24, 56, 64, 72, 80
ALL_PHASES = ("p1", "p2", "p3a", "p3b", "p3c", "p4")


def build(phases=ALL_PHASES, dbg=()):
    nc = bass.Bass("TRN2", target_bir_lowering=False)
    S = Sched()

    def dram(name, shape, dt, kind=None):
        if kind is None and name in dbg:
            kind = "ExternalOutput"
        if kind is None:
            return nc.dram_tensor(name, shape, dt).ap()
        return nc.dram_tensor(name, shape, dt, kind=kind).ap()

    x_d = dram("x", [S_LEN, D], F32, "ExternalInput")
    gbc_d = dram("gbc", [8 * 128, D], F32, "ExternalInput")
    vec_d = dram("vec", [128, NVEC], F32, "ExternalInput")
    cst_d = dram("cst", [128, 5 * 128], F32, "ExternalInput")
    wbd_d = dram("wbd", [128, 16 * 128], F32, "ExternalInput")
    wf = {k: dram(k + "_f", list(v), F32, "ExternalInput") for k, v in WSPEC.items()}
    wb = {k: dram(k + "_b", list(v), BF16) for k, v in WSPEC.items()}
    y_d = dram("y", [S_LEN, D], F32, "ExternalOutput")
    xA = dram("xA", [S_LEN, D], F32)
    xB = dram("xB", [S_LEN, D], F32)
    xC = dram("xC", [S_LEN, D], F32)
    qT_s = dram("qT_s", [16 * 128, S_LEN], BF16)
    kT_s = dram("kT_s", [16 * 128, S_LEN], BF16)
    v_s = dram("v_s", [S_LEN, D], BF16)
    aT_s = dram("aT_s", [D, S_LEN], BF16)

    with contextlib.ExitStack() as top:
        _uid = [0]

        def sbt(st, n, sh, dt):
            _uid[0] += 1
            return st.enter_context(nc.sbuf_tensor("sb%d_%s" % (_uid[0], n), sh, dt))
        PB = [top.enter_context(nc.psum_tensor(f"pb{i}", [128, 512], F32)) for i in range(6)]
        TP = [top.enter_context(nc.psum_tensor(f"tp{i}", [128, 1024], BF16)) for i in range(2)]
        bankring = Ring("ps", 6)
        cstf = sbt(top, "cstf", [128, 5 * 128], F32)
        cstb = sbt(top, "cstb", [128, 5 * 128], BF16)
        vec = sbt(top, "vec", [128, NVEC], F32)
        epst = sbt(top, "epst", [128, 1], F32)
        onet = sbt(top, "onet", [128, 1], F32)
        S.bar_t = sbt(top, "bar_t", [128, 1], F32)
        ident = cstb[:, 0:128]
        ntri = cstb[:, 128:256]
        nones = cstb[:, 256:384]
        negm = cstb[:, 384:512]
        zeros = cstb[:, 512:640]

        S.op("sp", lambda e: e.dma_start(out=cstf[:], in_=cst_d), writes=["cstf"], dma="c0")
        S.op("sp", lambda e: e.dma_start(out=vec[:], in_=vec_d), writes=["vec"], dma="c1")
        S.op("dve", lambda e: e.tensor_copy(out=cstb[:], in_=cstf[:]), reads=["cstf"], writes=["cstb"])
        S.op("dve", lambda e: e.memset(epst[:], EPS), writes=["epst"])
        S.op("dve", lambda e: e.memset(onet[:], 1.0), writes=["onet"])
        cast_jobs = []

        def add_cast(k):
            rows, cols = WSPEC[k]
            step = (8 << 20) // (cols * 4)
            for r0 in range(0, rows, step):
                cast_jobs.append((k, r0, min(rows, r0 + step), r0 // step))

        def issue_casts(n):
            for _ in range(n):
                if not cast_jobs:
                    return
                k, r0, r1, ci = cast_jobs.pop(0)
                sem = "cast_%s_%d" % (k, ci) if k in ("win", "wout") else "cast_" + k
                S.op("pool", lambda e, k=k, r0=r0, r1=r1: e.dma_start(out=wb[k][r0:r1, :], in_=wf[k][r0:r1, :]),
                     writes=[("wb", k, ci)], dma=sem)

        for ph, ws in (("p1", ["win", "wout"]), ("p2", ["wup0", "wdn0"]), ("p3a", ["wqk", "wv"]), ("p3c", ["wo"]), ("p4", ["wup1", "wdn1"])):
            if ph in phases:
                for k in ws:
                    add_cast(k)
        issue_casts(7 if "p1" in phases else 4)

        def wres_rows(k, r0, r1):
            step = (8 << 20) // (WSPEC[k][1] * 4)
            return [("wb", k, i) for i in range(r0 // step, (r1 - 1) // step + 1)]

        def prep_tile(st_bufs, xsrc, t, gidx, hname="hT", part="all", ctx=None, tbs=(0, 1, 2, 3)):
            xin, xring, hbf, hring, hT, gbc, small = st_bufs
            if ctx is None:
                ctx = {}
            if part == "L":
                for tb in tbs:
                    xs, xres = xring.next()
                    ctx[("x", tb)] = (xs, xres)
                    r0 = t * T + tb * 128
                    S.op("act", lambda e, xs=xs, r0=r0: e.dma_start(out=xin[:, xs, :], in_=xsrc[r0:r0 + 128, :]),
                         reads=[("xd", id(xsrc), r0 // 128)], writes=[xres], dma="xin%d" % xs)
                return ctx
            if part in ("all", "A"):
                S.op("dve", lambda e: e.memset(small[:, tbs[0]:tbs[-1] + 1], 0.0), writes=[("ssA", i) for i in tbs])
                if len(tbs) <= 2:
                    for tb in tbs:
                        if ("x", tb) in ctx:
                            continue
                        xs, xres = xring.next()
                        ctx[("x", tb)] = (xs, xres)
                        r0 = t * T + tb * 128
                        S.op("act", lambda e, xs=xs, r0=r0: e.dma_start(out=xin[:, xs, :], in_=xsrc[r0:r0 + 128, :]),
                             reads=[("xd", id(xsrc), r0 // 128)], writes=[xres], dma="xin%d" % xs)
            for tb in tbs:
                if part in ("all", "A"):
                    hs, hres = hring.next()
                    ctx[tb] = (hs, hres)
                    r0 = t * T + tb * 128
                    if ("x", tb) in ctx:
                        xs, xres = ctx[("x", tb)]
                    else:
                        xs, xres = xring.next()
                        S.op("act", lambda e, xs=xs, r0=r0: e.dma_start(out=xin[:, xs, :], in_=xsrc[r0:r0 + 128, :]),
                             reads=[("xd", id(xsrc), r0 // 128)], writes=[xres], dma="xin%d" % xs)
                    ss = small[:, tb:tb + 1]
                    sd = small[:, 4 + tb:5 + tb]
                    rs = small[:, 8 + tb:9 + tb]
                    S.op("act", lambda e, xs=xs, hs=hs, ss=ss: e.activation(out=hbf[:, hs, :], in_=xin[:, xs, :], func=AF.Square, accum_out=ss),
                         reads=[xres, ("ssA", tb)], writes=[hres, ("ssA", tb)])
                    S.op("act", lambda e, ss=ss, sd=sd: e.activation(out=sd, in_=ss, func=AF.Sqrt, bias=epst[:], scale=1.0 / D),
                         reads=[("ssA", tb), "epst"], writes=[("sdA", tb)])
                    S.op("dve", lambda e, sd=sd, rs=rs: e.reciprocal(out=rs, in_=sd), reads=[("sdA", tb)], writes=[("rsA", tb)])
                    S.op("dve", lambda e, xs=xs, hs=hs, rs=rs: e.scalar_tensor_tensor(out=hbf[:, hs, :], in0=xin[:, xs, :], scalar=rs, in1=gbc[:],
                                                                                     op0=ALU.mult, op1=ALU.mult),
                         reads=[xres, ("rsA", tb), "gpre"], writes=[hres])
                if part in ("all", "B"):
                    hs, hres = ctx[tb]
                    for half in range(2):
                        for k in range(8):
                            kc = half * 8 + k
                            S.op("pe", lambda e, half=half, k=k, kc=kc, hs=hs: e.transpose(out=TP[half][:, k * 128:(k + 1) * 128],
                                                                                           in_=hbf[:, hs, kc * 128:(kc + 1) * 128], identity=ident),
                                 reads=[hres, "cstb"], writes=[("tp", half)])
                        dst = hT[:, half * 8:(half + 1) * 8, tb * 128:(tb + 1) * 128]
                        src = TP[half][:].rearrange("p (k t) -> p k t", k=8)
                        if half == 0:
                            S.op("act", lambda e, dst=dst, src=src: e.activation(out=dst, in_=src, func=AF.Copy),
                                 reads=[("tp", half)], writes=[(hname, tb)])
                        else:
                            S.op("dve", lambda e, dst=dst, src=src: e.tensor_copy(out=dst, in_=src),
                                 reads=[("tp", half)], writes=[(hname, tb)])
            return ctx

        def gemm_feat(rhs_of, rhs_res, nkc, wname, nunits, G, wbuf, wring, evac):
            for u0 in range(0, nunits, G):
                ws, wr = wring.next()
                S.op("sp", lambda e, ws=ws, u0=u0: e.dma_start(out=wbuf[:, ws, :, :],
                                                               in_=wb[wname][u0 * 128:(u0 + G) * 128, :].rearrange("(g p) l -> p g l", p=128)),
                     reads=wres_rows(wname, u0 * 128, (u0 + G) * 128), writes=[wr], dma="wf%d" % ws)
                for g in range(G):
                    b, br = bankring.next()
                    for kc in range(nkc):
                        S.op("pe", lambda e, b=b, ws=ws, g=g, kc=kc: e.matmul(PB[b][:], lhsT=wbuf[:, ws, g, kc * 128:(kc + 1) * 128],
                                                                               rhs=rhs_of(kc), start=(kc == 0), stop=(kc == nkc - 1)),
                             reads=[wr] + rhs_res, writes=[br])
                    evac(u0 + g, b, br)

        def gemm_tok(lhs_of, lhs_res, nkc, wname, wbuf, wring, evac, ks_order=None):
            nks = nkc // 8
            for nb in range(4):
                banks = [bankring.next() for _ in range(4)]
                korder = list(ks_order) if ks_order is not None else list(range(nks))
                for ki, ks in enumerate(korder):
                    ws, wr = wring.next()
                    blk = nb * nks + ks
                    S.op("sp", lambda e, ws=ws, blk=blk: e.dma_start(out=wbuf[:, ws, :], in_=wb[wname][blk * 128:(blk + 1) * 128, :]),
                         reads=wres_rows(wname, blk * 128, (blk + 1) * 128), writes=[wr], dma="wt%d" % ws)
                    for tb in range(4):
                        b, br = banks[tb]
                        for k in range(8):
                            kc = ks * 8 + k
                            S.op("pe", lambda e, b=b, ws=ws, k=k, kc=kc, tb=tb, ki=ki: e.matmul(PB[b][:], lhsT=lhs_of(kc)[:, tb * 128:(tb + 1) * 128],
                                                                                         rhs=wbuf[:, ws, k * 512:(k + 1) * 512],
                                                                                         start=(ki == 0 and k == 0), stop=(ki == nks - 1 and k == 7)),
                                 reads=[wr] + lhs_res, writes=[br])
                for tb in range(4):
                    b, br = banks[tb]
                    evac(tb, nb, b, br)

        def post_init(xsrc, xdst, t):
            r0 = t * T
            S.op("pool", lambda e: e.dma_start(out=xdst[r0:r0 + T, :], in_=xsrc[r0:r0 + T, :]),
                 reads=[("xd", id(xsrc), r0 // 128 + i) for i in range(4)], writes=[("xd", id(xdst), r0 // 128 + i) for i in range(4)], dma="xcp")

        def post_tile(mix, xin, xring, gbc, small, junk, xsrc, xdst, t, gidx, final=False):
            S.op("dve", lambda e: e.memset(small[:, 12:16], 0.0), writes=[("ssB", i) for i in range(4)])
            for tb in range(4):
                r0 = t * T + tb * 128
                ss = small[:, 12 + tb:13 + tb]
                sd = small[:, 16 + tb:17 + tb]
                rs = small[:, 20 + tb:21 + tb]
                mres = ("mix", tb)
                S.op("act", lambda e, tb=tb, ss=ss: e.activation(out=junk[:], in_=mix[:, tb, :], func=AF.Square, accum_out=ss),
                     reads=[mres, ("ssB", tb)], writes=["junk", ("ssB", tb)])
                S.op("act", lambda e, ss=ss, sd=sd: e.activation(out=sd, in_=ss, func=AF.Sqrt, bias=epst[:], scale=1.0 / D),
                     reads=[("ssB", tb), "epst"], writes=[("sdB", tb)])
                S.op("dve", lambda e, sd=sd, rs=rs: e.reciprocal(out=rs, in_=sd), reads=[("sdB", tb)], writes=[("rsB", tb)])
                S.op("dve", lambda e, tb=tb, rs=rs: e.scalar_tensor_tensor(out=mix[:, tb, :], in0=mix[:, tb, :], scalar=rs, in1=gbc[:],
                                                                           op0=ALU.mult, op1=ALU.mult),
                     reads=[mres, ("rsB", tb), "gpost"], writes=[mres])
                S.op("pool", lambda e, tb=tb, r0=r0: e.dma_start(out=xdst[r0:r0 + 128, :], in_=mix[:, tb, :], accum_op=ALU.add),
                     reads=[mres], writes=[("xd", id(xdst), r0 // 128)], dma="xout%d" % tb)

        def load_gains(gpre, gi_pre, gpost, gi_post):
            if gpre is not None:
                S.op("sp", lambda e: e.dma_start(out=gpre[:], in_=gbc_d[gi_pre * 128:(gi_pre + 1) * 128, :]), writes=["gpre"], dma="gbc")
            if gpost is not None:
                S.op("sp", lambda e: e.dma_start(out=gpost[:], in_=gbc_d[gi_post * 128:(gi_post + 1) * 128, :]), writes=["gpost"], dma="gbc")

        def phase_p1():
            with contextlib.ExitStack() as st:
                xin = sbt(st, "xin", [128, 2, D], F32)
                hbf = sbt(st, "hbf", [128, 4, D], BF16)
                hT = sbt(st, "hT", [128, NKC, T], BF16)
                yT = sbt(st, "yT", [128, NKC, T], BF16)
                mix = sbt(st, "mix", [128, 4, D], F32)
                gbc = sbt(st, "gpre", [128, D], F32)
                gpost = sbt(st, "gpost", [128, D], F32)
                small = sbt(st, "small", [128, 32], F32)
                wfb = sbt(st, "wfb", [128, 2, 3, D], BF16)
                wtb = sbt(st, "wtb", [128, 2, 4096], BF16)
                uev = sbt(st, "uev", [128, 4, T], F32)
                cx = sbt(st, "cx", [128, 1, T + 2], F32)
                ycv = sbt(st, "ycv", [128, 1, T], F32)
                bxh = sbt(st, "bxh", [128, 1, T + 3], F32)
                xr = sbt(st, "xr", [128, 2, T], F32)
                xrb = sbt(st, "xrb", [128, 2, T], BF16)
                tr_ = sbt(st, "tr_", [128, 1, T], F32)
                ti_ = sbt(st, "ti_", [128, 1, T], F32)
                ta_ = sbt(st, "ta_", [128, 1, T], F32)
                tm_ = sbt(st, "tm_", [128, 1, T], F32)
                tb_ = sbt(st, "tb_", [128, 1, T], F32)
                hs_ = sbt(st, "hs_", [128, 1, T], F32)
                gl_ = sbt(st, "gl_", [128, 1, T], F32)
                gu_ = sbt(st, "gu_", [128, 1, T], F32)
                haloA = sbt(st, "haloA", [128, 8, 2], F32)
                haloB = sbt(st, "haloB", [128, 8, 3], F32)
                hstate = sbt(st, "hstate", [128, 8], F32)
                cl = sbt(st, "cl", [128, 8], F32)
                wbdb = sbt(st, "wbdb", [128, 16 * 128], BF16)
                junkp = sbt(st, "junkp", [128, D], BF16)
                xring, hring = Ring("xin", 2), Ring("hbf", 4)
                wfring, wtring, uring = Ring("wfb", 2), Ring("wtb", 2), Ring("uev", 4)
                r2 = {n: Ring(n, 2) for n in ("cx", "ycv", "bxh", "xr", "xrb", "tr", "ti", "ta", "tm", "tb", "hs", "gl", "gu")}
                for n in ("gu", "tm", "tb", "tr", "ti", "gl", "ta", "cx", "ycv", "bxh", "hs"):
                    r2[n] = Ring(n, 1)

                S.op("pool", lambda e: e.dma_start(out=wbdb[:], in_=wbd_d), writes=["wbdb"], dma="c2")
                S.op("dve", lambda e: e.memset(haloA[:], 0.0), writes=["haloA"])
                S.op("dve", lambda e: e.memset(haloB[:], 0.0), writes=["haloB"])
                S.op("dve", lambda e: e.memset(hstate[:], 0.0), writes=["hstate"])
                S.op("act", lambda e: e.activation(out=cl[:], in_=vec[:, V_LAM:V_LAM + 8], func=AF.Exp, scale=-1.0), reads=["vec"], writes=["cl"])
                S.op("act", lambda e: e.activation(out=cl[:], in_=cl[:], func=AF.Ln, bias=onet[:], scale=1.0), reads=["cl", "onet"], writes=["cl"])
                S.op("dve", lambda e: e.tensor_scalar(out=cl[:], in0=cl[:], scalar1=-8.0, scalar2=None, op0=ALU.mult), reads=["cl"], writes=["cl"])

                bufs = (xin, xring, hbf, hring, hT, gbc, small)
                load_gains(gbc, 0, gpost, 1)
                prep_tile(bufs, x_d, 0, 0)
                for t in range(NT):
                    issue_casts(2)
                    post_init(x_d, xA, t)
                    ev = {}
                    bctx = {}

                    def evac_in(u, b, br):
                        c, r = divmod(u, 5)
                        kind = (3, 4, 0, 1, 2)[r]
                        if kind == 4:
                            s, res = r2["bxh"].next()
                            S.op("pool", lambda e, s=s, c=c: e.tensor_copy(out=bxh[:, s, 0:3], in_=haloB[:, c, :]), reads=["haloB"], writes=[res])
                            S.op("act", lambda e, s=s, b=b: e.activation(out=bxh[:, s, 3:T + 3], in_=PB[b][:], func=AF.Copy), reads=[br], writes=[res])
                            S.op("pool", lambda e, s=s, c=c: e.tensor_copy(out=haloB[:, c, :], in_=bxh[:, s, T:T + 3]), reads=[res], writes=["haloB"])
                            ev[kind] = (bxh[:, s, :], res)
                        else:
                            s, res = uring.next()
                            S.op("act", lambda e, s=s, b=b: e.activation(out=uev[:, s, :], in_=PB[b][:], func=AF.Copy), reads=[br], writes=[res])
                            ev[kind] = (uev[:, s, :], res)
                        if kind == 2:
                            mixer_a(c)
                            mixer_b2(c)
                        if kind == 4:
                            mixer_b1(c)

                    def mixer_a(c):
                        (bg, bgr), (cg, cgr), (ax, axr) = ev[0], ev[1], ev[2]
                        s, cres = r2["cx"].next()
                        ys, yres = r2["ycv"].next()
                        w = lambda k: vec[:, V_CA + c * 3 + k:V_CA + c * 3 + k + 1]
                        S.op("pool", lambda e: e.tensor_copy(out=cx[:, s, 0:2], in_=haloA[:, c, :]), reads=["haloA"], writes=[cres])
                        S.op("dve", lambda e: e.tensor_tensor(out=cx[:, s, 2:T + 2], in0=cg, in1=ax, op=ALU.mult), reads=[cgr, axr], writes=[cres])
                        S.op("pool", lambda e: e.tensor_copy(out=haloA[:, c, :], in_=cx[:, s, T:T + 2]), reads=[cres], writes=["haloA"])
                        S.op("dve", lambda e: e.tensor_scalar(out=ycv[:, ys, :], in0=cx[:, s, 2:T + 2], scalar1=w(2), scalar2=None, op0=ALU.mult),
                             reads=[cres, "vec"], writes=[yres])
                        S.op("dve", lambda e: e.scalar_tensor_tensor(out=ycv[:, ys, :], in0=cx[:, s, 1:T + 1], scalar=w(1), in1=ycv[:, ys, :],
                                                                     op0=ALU.mult, op1=ALU.add), reads=[cres, yres], writes=[yres])
                        S.op("dve", lambda e: e.scalar_tensor_tensor(out=ycv[:, ys, :], in0=cx[:, s, 0:T], scalar=w(0), in1=ycv[:, ys, :],
                                                                     op0=ALU.mult, op1=ALU.add), reads=[cres, yres], writes=[yres])
                        S.op("pool", lambda e: e.tensor_tensor(out=yT[:, c, :], in0=bg, in1=ycv[:, ys, :], op=ALU.mult),
                             reads=[bgr, yres], writes=[("yT", c)])

                    def mixer_b1(c):
                        (g, gr), (bx, bxr) = ev[3], ev[4]
                        nx = {n: r2[n].next() for n in ("xr", "xrb")}
                        XR, XRB = xr[:, nx["xr"][0], :], xrb[:, nx["xrb"][0], :]
                        rr = {n: nx[n][1] for n in nx}
                        w = lambda k: vec[:, V_CB + c * 4 + k:V_CB + c * 4 + k + 1]
                        col = lambda base: vec[:, base + c:base + c + 1]
                        S.op("dve", lambda e: e.tensor_scalar(out=XR, in0=bx[:, 3:T + 3], scalar1=w(3), scalar2=col(V_CBB), op0=ALU.mult, op1=ALU.add),
                             reads=[bxr, "vec"], writes=[rr["xr"]])
                        for k in (2, 1, 0):
                            S.op("dve", lambda e, k=k: e.scalar_tensor_tensor(out=XR, in0=bx[:, k:T + k], scalar=w(k), in1=XR, op0=ALU.mult, op1=ALU.add),
                                 reads=[bxr, rr["xr"]], writes=[rr["xr"]])
                        S.op("pool", lambda e: e.tensor_copy(out=XRB, in_=XR), reads=[rr["xr"]], writes=[rr["xrb"]])
                        bctx[c] = (g, gr, nx)

                    def mixer_b2(c):
                        g, gr, nx0 = bctx[c]
                        nx = {n: r2[n].next() for n in ("tr", "ti", "ta", "tm", "tb", "hs", "gl", "gu")}
                        nx.update(nx0)
                        sl = lambda buf, n: buf[:, nx[n][0], :]
                        w = lambda k: vec[:, V_CB + c * 4 + k:V_CB + c * 4 + k + 1]
                        col = lambda base: vec[:, base + c:base + c + 1]
                        XR, XRB, R, I, A, M, Bv, H, GL, GU = (sl(xr, "xr"), sl(xrb, "xrb"), sl(tr_, "tr"), sl(ti_, "ti"), sl(ta_, "ta"),
                                                              sl(tm_, "tm"), sl(tb_, "tb"), sl(hs_, "hs"), sl(gl_, "gl"), sl(gu_, "gu"))
                        rr = {n: nx[n][1] for n in nx}
                        b1, br1 = bankring.next()
                        b2, br2 = bankring.next()
                        S.op("pe", lambda e: e.matmul(PB[b1][:], lhsT=wbdb[:, c * 128:(c + 1) * 128], rhs=XRB, start=True, stop=True),
                             reads=[rr["xrb"], "wbdb"], writes=[br1])
                        S.op("pe", lambda e: e.matmul(PB[b2][:], lhsT=wbdb[:, (8 + c) * 128:(9 + c) * 128], rhs=XRB, start=True, stop=True),
                             reads=[rr["xrb"], "wbdb"], writes=[br2])
                        S.op("act", lambda e: e.activation(out=R, in_=PB[b1][:], func=AF.Sigmoid, bias=col(V_BA), scale=1.0), reads=[br1, "vec"], writes=[rr["tr"]])
                        S.op("act", lambda e: e.activation(out=I, in_=PB[b2][:], func=AF.Sigmoid, bias=col(V_BX), scale=1.0), reads=[br2, "vec"], writes=[rr["ti"]])
                        S.op("pool", lambda e: e.tensor_tensor(out=GU, in0=g, in1=g, op=ALU.mult), reads=[gr], writes=[rr["gu"]])
                        S.op("pool", lambda e: e.tensor_scalar(out=GU, in0=GU, scalar1=0.044715, scalar2=1.0, op0=ALU.mult, op1=ALU.add),
                             reads=[rr["gu"]], writes=[rr["gu"]])
                        S.op("pool", lambda e: e.tensor_tensor(out=GU, in0=GU, in1=g, op=ALU.mult), reads=[rr["gu"], gr], writes=[rr["gu"]])
                        S.op("act", lambda e: e.activation(out=GL, in_=GU, func=AF.Sigmoid, scale=1.5957691216), reads=[rr["gu"]], writes=[rr["gl"]])
                        S.op("pool", lambda e: e.tensor_tensor(out=GL, in0=GL, in1=g, op=ALU.mult), reads=[rr["gl"], gr], writes=[rr["gl"]])
                        S.op("act", lambda e: e.activation(out=A, in_=R, func=AF.Exp, scale=cl[:, c:c + 1]), reads=[rr["tr"], "cl"], writes=[rr["ta"]])
                        S.op("pool", lambda e: e.tensor_tensor(out=M, in0=A, in1=A, op=ALU.mult), reads=[rr["ta"]], writes=[rr["tm"]])
                        S.op("act", lambda e: e.activation(out=M, in_=M, func=AF.Sqrt, bias=onet[:], scale=-1.0), reads=[rr["tm"], "onet"], writes=[rr["tm"]])
                        S.op("dve", lambda e: e.tensor_tensor(out=Bv, in0=I, in1=XR, op=ALU.mult), reads=[rr["ti"], rr["xr"]], writes=[rr["tb"]])
                        S.op("dve", lambda e: e.tensor_tensor(out=Bv, in0=Bv, in1=M, op=ALU.mult), reads=[rr["tb"], rr["tm"]], writes=[rr["tb"]])
                        S.op("dve", lambda e: e.tensor_tensor_scan(out=H, data0=A, data1=Bv, initial=hstate[:, c:c + 1], op0=ALU.mult, op1=ALU.add),
                             reads=[rr["ta"], rr["tb"], "hstate"], writes=[rr["hs"]])
                        S.op("dve", lambda e: e.tensor_copy(out=hstate[:, c:c + 1], in_=H[:, T - 1:T]), reads=[rr["hs"]], writes=["hstate"])
                        S.op("dve", lambda e: e.tensor_tensor(out=yT[:, 8 + c, :], in0=H, in1=GL, op=ALU.mult), reads=[rr["hs"], rr["gl"]], writes=[("yT", 8 + c)])

                    pctx = {}

                    def hook0(t=t, pctx=pctx):
                        if t + 1 < NT:
                            prep_tile(bufs, x_d, t + 1, 0, part="L", ctx=pctx, tbs=(0, 1))

                    def hook1(t=t, pctx=pctx):
                        if t > 0:
                            post_tile(mix, xin, xring, gpost, small, junkp, x_d, xA, t - 1, 1)
                        if t + 1 < NT:
                            prep_tile(bufs, x_d, t + 1, 0, part="A", ctx=pctx, tbs=(0, 1))
                            prep_tile(bufs, x_d, t + 1, 0, part="L", ctx=pctx, tbs=(2, 3))

                    def hook2(t=t, pctx=pctx):
                        if t + 1 < NT:
                            prep_tile(bufs, x_d, t + 1, 0, part="A", ctx=pctx, tbs=(2, 3))
                    _gemm_feat_groups(hT, "win", [2, 3] * 8, wfb, wfring, evac_in, mid_hook=[(1, hook0), (5, hook1), (11, hook2)])
                    if t + 1 < NT:
                        prep_tile(bufs, x_d, t + 1, 0, part="B", ctx=pctx)

                    def evac_out(tb, nb, b, br):
                        S.op("act", lambda e: e.activation(out=mix[:, tb, nb * 512:(nb + 1) * 512], in_=PB[b][:], func=AF.Copy),
                             reads=[br], writes=[("mix", tb)])
                    gemm_tok(lambda kc: yT[:, kc, :], [("yT", i) for i in range(16)], NKC, "wout", wtb, wtring, evac_out)
                    if t == NT - 1:
                        post_tile(mix, xin, xring, gpost, small, junkp, x_d, xA, t, 1)

        def _gemm_feat_groups(hT, wname, groups, wbuf, wring, evac, nkc=NKC, hname="hT", mid_hook=None):
            u0 = 0
            for gi, G in enumerate(groups):
                if mid_hook is not None:
                    for hk in (mid_hook if isinstance(mid_hook, list) else [mid_hook]):
                        if gi == hk[0]:
                            hk[1]()
                ws, wr = wring.next()
                S.op("sp", lambda e, ws=ws, u0=u0, G=G: e.dma_start(out=wbuf[:, ws, 0:G, :],
                                                                    in_=wb[wname][u0 * 128:(u0 + G) * 128, :].rearrange("(g p) l -> p g l", p=128)),
                     reads=wres_rows(wname, u0 * 128, (u0 + G) * 128), writes=[wr], dma="wf%d" % ws)
                for g in range(G):
                    b, br = bankring.next()
                    for kc in range(nkc):
                        S.op("pe", lambda e, b=b, ws=ws, g=g, kc=kc: e.matmul(PB[b][:], lhsT=wbuf[:, ws, g, kc * 128:(kc + 1) * 128],
                                                                               rhs=hT[:, kc, :], start=(kc == 0), stop=(kc == nkc - 1)),
                             reads=[wr] + [(hname, i) for i in range(4)], writes=[br])
                    evac(u0 + g, b, br)
                u0 += G

        def phase_mlp(layer, xsrc, xdst):
            with contextlib.ExitStack() as st:
                xin = sbt(st, "xin", [128, 2, D], F32)
                hbf = sbt(st, "hbf", [128, 4, D], BF16)
                hT = sbt(st, "hT", [128, NKC, T], BF16)
                uT = sbt(st, "uT", [128, 64, T], BF16)
                mix = sbt(st, "mix", [128, 4, D], F32)
                gbc = sbt(st, "gpre", [128, D], F32)
                gpost = sbt(st, "gpost", [128, D], F32)
                small = sbt(st, "small", [128, 32], F32)
                wfb = sbt(st, "wfb", [128, 2, 2, D], BF16)
                wtb = sbt(st, "wtb", [128, 2, 4096], BF16)
                rl = sbt(st, "rl", [128, 3, T], F32)
                junkp = sbt(st, "junkp", [128, D], BF16)
                xring, hring = Ring("xin", 2), Ring("hbf", 4)
                wfring, wtring, rring = Ring("wfb", 2), Ring("wtb", 2), Ring("rl", 3)
                bufs = (xin, xring, hbf, hring, hT, gbc, small)
                load_gains(gbc, layer * 4 + 2, gpost, layer * 4 + 3)
                prep_tile(bufs, xsrc, 0, layer * 4 + 2)
                for t in range(NT):
                    issue_casts(1 if layer == 0 else 100)
                    post_init(xsrc, xdst, t)

                    def evac_up(u, b, br):
                        s, res = rring.next()
                        S.op("act", lambda e: e.activation(out=rl[:, s, :], in_=PB[b][:], func=AF.Relu), reads=[br], writes=[res])
                        eng = "pool" if u % 2 == 0 else "dve"
                        S.op(eng, lambda e: e.tensor_tensor(out=uT[:, u, :], in0=rl[:, s, :], in1=rl[:, s, :], op=ALU.mult),
                             reads=[res], writes=[("uT", u)])
                    pctx = {}
                    def mh1(t=t, pctx=pctx):
                        prep_tile(bufs, xsrc, t + 1, layer * 4 + 2, part="A", ctx=pctx, tbs=(0, 1))
                        prep_tile(bufs, xsrc, t + 1, layer * 4 + 2, part="L", ctx=pctx, tbs=(2, 3))
                    hook = [(2, lambda t=t, pctx=pctx: prep_tile(bufs, xsrc, t + 1, layer * 4 + 2, part="L", ctx=pctx, tbs=(0, 1))),
                            (10, mh1),
                            (20, lambda t=t, pctx=pctx: prep_tile(bufs, xsrc, t + 1, layer * 4 + 2, part="A", ctx=pctx, tbs=(2, 3)))] if t + 1 < NT else None
                    _gemm_feat_groups(hT, "wup%d" % layer, [2] * 32, wfb, wfring, evac_up, mid_hook=hook)
                    if t + 1 < NT:
                        prep_tile(bufs, xsrc, t + 1, layer * 4 + 2, part="B", ctx=pctx)

                    def evac_dn(tb, nb, b, br):
                        S.op("act", lambda e: e.activation(out=mix[:, tb, nb * 512:(nb + 1) * 512], in_=PB[b][:], func=AF.Copy),
                             reads=[br], writes=[("mix", tb)])
                    gemm_tok(lambda kc: uT[:, kc, :], [("uT", i) for i in range(64)], 64, "wdn%d" % layer, wtb, wtring, evac_dn)
                    post_tile(mix, xin, xring, gpost, small, junkp, xsrc, xdst, t, layer * 4 + 3)

        def phase_p3a():
            with contextlib.ExitStack() as st:
                xin = sbt(st, "xin", [128, 2, D], F32)
                hbf = sbt(st, "hbf", [128, 4, D], BF16)
                hT2 = sbt(st, "hT2", [128, 2, NKC, T], BF16)
                gbc = sbt(st, "gpre", [128, D], F32)
                small = sbt(st, "small", [128, 32], F32)
                wfb = sbt(st, "wfb", [128, 2, 2, D], BF16)
                wtb = sbt(st, "wtb", [128, 3, 4096], BF16)
                qst = sbt(st, "qst", [128, 4, T], BF16)
                vst = sbt(st, "vst", [128, 4, D], BF16)
                xring, hring = Ring("xin", 2), Ring("hbf", 4)
                wfring, wtring, qring = Ring("wfb", 2), Ring("wtb", 3), Ring("qst", 4)
                qscale = 1.0 / math.sqrt(128.0)
                load_gains(gbc, 4, None, 0)
                mkbufs = lambda t: (xin, xring, hbf, hring, hT2[:, t % 2, :, :], gbc, small)
                prep_tile(mkbufs(0), xB, 0, 4, hname="hT0")
                for t in range(NT):
                    issue_casts(1)
                    hT = hT2[:, t % 2, :, :]
                    hname = "hT%d" % (t % 2)

                    def evac_qk(u, b, br):
                        s, res = qring.next()
                        if u < 16:
                            S.op("act", lambda e: e.activation(out=qst[:, s, :], in_=PB[b][:], func=AF.Copy, scale=qscale), reads=[br], writes=[res])
                            dst = qT_s[u * 128:(u + 1) * 128, t * T:(t + 1) * T]
                            dres = ("qT", u, t)
                        else:
                            S.op("dve", lambda e: e.tensor_copy(out=qst[:, s, :], in_=PB[b][:]), reads=[br], writes=[res])
                            dst = kT_s[(u - 16) * 128:(u - 15) * 128, t * T:(t + 1) * T]
                            dres = ("kT", u - 16, t)
                        S.op("pool", lambda e: e.dma_start(out=dst, in_=qst[:, s, :]), reads=[res], writes=[dres], dma="qst%d" % s)
                    pctx = {}
                    def qh1(t=t, pctx=pctx):
                        prep_tile(mkbufs(t + 1), xB, t + 1, 4, hname="hT%d" % ((t + 1) % 2), part="A", ctx=pctx, tbs=(0, 1))
                        prep_tile(mkbufs(t + 1), xB, t + 1, 4, hname="hT%d" % ((t + 1) % 2), part="L", ctx=pctx, tbs=(2, 3))
                    hook = [(1, lambda t=t, pctx=pctx: prep_tile(mkbufs(t + 1), xB, t + 1, 4, hname="hT%d" % ((t + 1) % 2), part="L", ctx=pctx, tbs=(0, 1))),
                            (5, qh1),
                            (11, lambda t=t, pctx=pctx: prep_tile(mkbufs(t + 1), xB, t + 1, 4, hname="hT%d" % ((t + 1) % 2), part="A", ctx=pctx, tbs=(2, 3)))] if t + 1 < NT else None
                    _gemm_feat_groups(hT, "wqk", [2] * 16, wfb, wfring, evac_qk, hname=hname, mid_hook=hook)
                    if t + 1 < NT:
                        prep_tile(mkbufs(t + 1), xB, t + 1, 4, hname="hT%d" % ((t + 1) % 2), part="B", ctx=pctx)

                    def evac_v(tb, nb, b, br):
                        S.op("act", lambda e: e.activation(out=vst[:, tb, nb * 512:(nb + 1) * 512], in_=PB[b][:], func=AF.Copy),
                             reads=[br], writes=[("vst", tb)])
                        if nb == 3:
                            r0 = t * T + tb * 128
                            S.op("pool", lambda e: e.dma_start(out=v_s[r0:r0 + 128, :], in_=vst[:, tb, :]),
                                 reads=[("vst", tb)], writes=[("v", r0 // 128)], dma="vst%d" % tb)
                    gemm_tok(lambda kc, hT=hT: hT[:, kc, :], [(hname, i) for i in range(4)], NKC, "wv", wtb, wtring, evac_v)

        def phase_p3b():
            with contextlib.ExitStack() as st:
                qT = sbt(st, "qT", [128, 2, S_LEN], BF16)
                kT = sbt(st, "kT", [128, 2, S_LEN], BF16)
                vv = sbt(st, "vv", [128, 2, 32, 128], BF16)
                et = sbt(st, "et", [128, 2, T], F32)
                spt = sbt(st, "spt", [128, 4, T], BF16)
                acc = sbt(st, "acc", [128, 6, T], BF16)
                wT = sbt(st, "wT", [128, 4, T], BF16)
                ost = sbt(st, "ost", [128, 2, T], BF16)
                hring = Ring("hd", 2)
                ering, sring, wring_, oring = Ring("et", 2), Ring("spt", 4), Ring("wT", 4), Ring("ost", 2)
                aring, bring = Ring("psA", 2), Ring("psB", 2)
                tiles = []
                sbi = 0
                for h in range(16):
                    hsl, hres = hring.next()
                    for I in range(NT):
                        ob = 4 + (sbi % 2)
                        aset = 3 * (sbi % 2)
                        sbi += 1
                        jtop = 4 * I + 3
                        for pos, j in enumerate(range(jtop, -1, -1)):
                            c0 = max(0, j - 4 * I)
                            tiles.append(dict(h=h, hsl=hsl, hres=hres, I=I, j=j, lo=c0 * 128, diag=(j >= 4 * I), first=(j == jtop), last=(j == 0),
                                              ob=ob, a_old=aset + pos % 3, a_new=aset + (pos + 1) % 3, aset=aset, newhead=(I == 0 and j == jtop)))

                def load_head(tl):
                    h, hsl, hres = tl["h"], tl["hsl"], tl["hres"]
                    if h < 8:
                        issue_casts(1)
                    S.op("sp", lambda e: e.dma_start(out=qT[:, hsl, :], in_=qT_s[h * 128:(h + 1) * 128, :]),
                         reads=[("qT", h, t) for t in range(NT)], writes=[hres], dma="hq%d" % hsl)
                    S.op("sp", lambda e: e.dma_start(out=kT[:, hsl, :], in_=kT_s[h * 128:(h + 1) * 128, :]),
                         reads=[("kT", h, t) for t in range(NT)], writes=[hres], dma="hq%d" % hsl)
                    S.op("sp", lambda e: e.dma_start(out=vv[:, hsl, :, :], in_=v_s[:, h * 128:(h + 1) * 128].rearrange("(j p) d -> p j d", p=128)),
                         reads=[("v", i) for i in range(32)], writes=[hres], dma="hq%d" % hsl)

                def stage1(tl):
                    hsl, hres, I, j, lo, diag = tl["hsl"], tl["hres"], tl["I"], tl["j"], tl["lo"], tl["diag"]
                    ob, obr = tl["ob"], ("ps", tl["ob"])
                    cs = slice(lo, T)
                    qs = slice(I * T + lo, (I + 1) * T)
                    ks = slice(j * 128, (j + 1) * 128)
                    if tl["newhead"]:
                        load_head(tl)
                    if tl["first"]:
                        S.op("pe", lambda e: e.matmul(PB[ob][:], lhsT=zeros, rhs=qT[:, hsl, 0:T], start=True, stop=False),
                             reads=["cstb", hres], writes=[obr])
                        for k in range(3):
                            ab = tl["aset"] + k
                            S.op("pool", lambda e, ab=ab: e.memset(acc[:, ab, :], 0.0), writes=[("acc", ab)])
                    ba, bar = aring.next()
                    es, eres = ering.next()
                    ss_, sres = sring.next()
                    tl["ss"], tl["sres"] = ss_, sres
                    S.op("pe", lambda e: e.matmul(PB[ba][:, cs], lhsT=kT[:, hsl, ks], rhs=qT[:, hsl, qs], start=True, stop=not diag),
                         reads=[hres], writes=[("ps", ba)])
                    if diag:
                        S.op("pe", lambda e: e.matmul(PB[ba][:, lo:lo + 128], lhsT=ident, rhs=negm, start=False, stop=True),
                             reads=["cstb"], writes=[("ps", ba)])
                    S.op("act", lambda e: e.activation(out=et[:, es, cs], in_=PB[ba][:, cs], func=AF.Exp), reads=[("ps", ba)], writes=[eres])
                    S.op("act", lambda e: e.activation(out=spt[:, ss_, cs], in_=et[:, es, cs], func=AF.Ln, bias=onet[:], scale=1.0),
                         reads=[eres, "onet"], writes=[sres])
                    if not tl["last"]:
                        a_old, a_new = tl["a_old"], tl["a_new"]
                        S.op("dve", lambda e: e.tensor_tensor(out=acc[:, a_new, cs], in0=acc[:, a_old, cs], in1=spt[:, ss_, cs], op=ALU.add),
                             reads=[("acc", a_old), sres], writes=[("acc", a_new)])

                def stage2(tl):
                    hsl, hres, I, j, lo, diag = tl["hsl"], tl["hres"], tl["I"], tl["j"], tl["lo"], tl["diag"]
                    cs = slice(lo, T)
                    qs = slice(I * T + lo, (I + 1) * T)
                    ks = slice(j * 128, (j + 1) * 128)
                    bb, _ = bring.next()
                    bb += 2
                    bbr = ("ps", bb)
                    ws_, wres_ = wring_.next()
                    tl["ws"], tl["wres"] = ws_, wres_
                    ss_, sres, a_old = tl["ss"], tl["sres"], tl["a_old"]
                    S.op("pe", lambda e: e.matmul(PB[bb][:, cs], lhsT=kT[:, hsl, ks], rhs=qT[:, hsl, qs], start=True, stop=False),
                         reads=[hres], writes=[bbr])
                    if diag:
                        S.op("pe", lambda e: e.matmul(PB[bb][:, lo:lo + 128], lhsT=ident, rhs=negm, start=False, stop=False),
                             reads=["cstb"], writes=[bbr])
                    if not tl["first"]:
                        S.op("pe", lambda e: e.matmul(PB[bb][:, cs], lhsT=nones, rhs=acc[:, a_old, cs], start=False, stop=False),
                             reads=["cstb", ("acc", a_old)], writes=[bbr])
                    S.op("pe", lambda e: e.matmul(PB[bb][:, cs], lhsT=ntri, rhs=spt[:, ss_, cs], start=False, stop=True),
                         reads=["cstb", sres], writes=[bbr])
                    S.op("act", lambda e: e.activation(out=wT[:, ws_, cs], in_=PB[bb][:, cs], func=AF.Exp), reads=[bbr], writes=[wres_])

                def stage3(tl):
                    hsl, hres, I, j, lo, h = tl["hsl"], tl["hres"], tl["I"], tl["j"], tl["lo"], tl["h"]
                    ob, obr = tl["ob"], ("ps", tl["ob"])
                    cs = slice(lo, T)
                    ws_, wres_ = tl["ws"], tl["wres"]
                    S.op("pe", lambda e: e.matmul(PB[ob][:, cs], lhsT=vv[:, hsl, j, :], rhs=wT[:, ws_, cs], start=False, stop=tl["last"]),
                         reads=[hres, wres_], writes=[obr])
                    if tl["last"]:
                        os_, ores = oring.next()
                        S.op("dve", lambda e: e.tensor_copy(out=ost[:, os_, :], in_=PB[ob][:]), reads=[obr], writes=[ores])
                        S.op("pool", lambda e: e.dma_start(out=aT_s[h * 128:(h + 1) * 128, I * T:(I + 1) * T], in_=ost[:, os_, :]),
                             reads=[ores], writes=[("aT", h, I)], dma="ost%d" % os_)

                N = len(tiles)
                for n in range(N + 2):
                    if n < N:
                        stage1(tiles[n])
                    if 0 <= n - 1 < N:
                        stage2(tiles[n - 1])
                    if 0 <= n - 2 < N:
                        stage3(tiles[n - 2])

        def phase_p3c():
            with contextlib.ExitStack() as st:
                xin = sbt(st, "xin", [128, 2, D], F32)
                aT = sbt(st, "aT", [128, 2, NKC, T], BF16)
                mix = sbt(st, "mix", [128, 4, D], F32)
                gbc = sbt(st, "gpost", [128, D], F32)
                junk = sbt(st, "junk", [128, D], BF16)
                small = sbt(st, "small", [128, 32], F32)
                wtb = sbt(st, "wtb", [128, 3, 4096], BF16)
                xring, wtring, aring = Ring("xin", 2), Ring("wtb", 3), Ring("aT", 2)
                issue_casts(100)
                load_gains(None, 0, gbc, 5)
                for t in range(NT):
                    post_init(xB, xC, t)
                    as_, ares = aring.next()
                    S.op("sp", lambda e, as_=as_, t=t: e.dma_start(out=aT[:, as_, :, :],
                                                                   in_=aT_s[:, t * T:(t + 1) * T].rearrange("(kc p) t -> p kc t", p=128)),
                         reads=[("aT", h, t) for h in range(16)], writes=[ares], dma="aT%d" % as_)

                    def evac_o(tb, nb, b, br):
                        S.op("act", lambda e: e.activation(out=mix[:, tb, nb * 512:(nb + 1) * 512], in_=PB[b][:], func=AF.Copy),
                             reads=[br], writes=[("mix", tb)])
                    gemm_tok(lambda kc, as_=as_: aT[:, as_, kc, :], [ares], NKC, "wo", wtb, wtring, evac_o)
                    post_tile(mix, xin, xring, gbc, small, junk, xB, xC, t, 5)

        if "p1" in phases:
            phase_p1()
            S.barrier()
        if "p2" in phases:
            phase_mlp(0, xA, xB)
            S.barrier()
        if "p3a" in phases:
            phase_p3a()
            S.barrier()
        if "p3b" in phases:
            phase_p3b()
            S.barrier()
        if "p3c" in phases:
            phase_p3c()
            S.barrier()
        if "p4" in phases:
            phase_mlp(1, xC, y_d)
        S.op("pool", lambda e: e.nop(), barrier=True)
        cnt = S.emit_all(nc, top)
    return nc, S, cnt


def _feat_units(W, order=None):
    K, N = W.shape
    nu = N // 128
    Wr = W.reshape(K // 128, 128, nu, 128)
    if order is not None:
        Wr = Wr[:, :, order, :]
    return np.ascontiguousarray(Wr.transpose(2, 1, 0, 3)).reshape(nu * 128, (K // 128) * 128)


def _tok_blocks(W):
    K, N = W.shape
    nks = K // 1024
    Wr = W.reshape(nks, 8, 128, N // 512, 512)
    return np.ascontiguousarray(Wr.transpose(3, 0, 2, 1, 4)).reshape((N // 512) * nks * 128, 8 * 512)


def _chunks(v):
    return np.ascontiguousarray(v.reshape(8, 128).T)


def prep_shared(inp):
    f = lambda a: np.ascontiguousarray(np.asarray(a, dtype=np.float32))
    g = f(inp["norm_gains"]).reshape(8, 1, D)
    sh = {}
    sh["gbc"] = np.ascontiguousarray(np.broadcast_to(g, (8, 128, D))).reshape(8 * 128, D)
    ca = f(inp["hyb_conv_a"])[0]
    cb = f(inp["hyb_conv_b"])[0]
    vec = np.zeros((128, NVEC), np.float32)
    vec[:, V_CA:V_CA + 24] = np.stack([_chunks(ca[k]) for k in range(3)], axis=2).reshape(128, 24)
    vec[:, V_CB:V_CB + 32] = np.stack([_chunks(cb[k]) for k in range(4)], axis=2).reshape(128, 32)
    vec[:, V_CBB:V_CBB + 8] = _chunks(f(inp["hyb_conv_b_bias"])[0])
    vec[:, V_BA:V_BA + 8] = _chunks(f(inp["hyb_rg_b_a"])[0])
    vec[:, V_BX:V_BX + 8] = _chunks(f(inp["hyb_rg_b_x"])[0])
    vec[:, V_LAM:V_LAM + 8] = _chunks(f(inp["hyb_rg_lambda"])[0])
    sh["vec"] = vec
    idx = np.arange(128)
    cst = np.zeros((128, 5 * 128), np.float32)
    cst[:, 0:128] = np.eye(128, dtype=np.float32)
    cst[:, 128:256] = -(idx[:, None] >= idx[None, :]).astype(np.float32)
    cst[:, 256:384] = -1.0
    cst[:, 384:512] = np.where(idx[:, None] >= idx[None, :], -30000.0, 0.0)
    sh["cst"] = cst
    wbd = np.zeros((128, 16, 128), np.float32)
    wa = f(inp["hyb_rg_w_a"])[0]
    wx = f(inp["hyb_rg_w_x"])[0]
    for c in range(8):
        for hh in range(2):
            wbd[hh * 64:(hh + 1) * 64, c, hh * 64:(hh + 1) * 64] = wa[2 * c + hh]
            wbd[hh * 64:(hh + 1) * 64, 8 + c, hh * 64:(hh + 1) * 64] = wx[2 * c + hh]
    sh["wbd"] = wbd.reshape(128, 16 * 128)
    order = []
    for c in range(8):
        order += [24 + c, 32 + c, c, 8 + c, 16 + c]
    sh["win_f"] = _feat_units(f(inp["hyb_w_in"])[0], order)
    sh["wout_f"] = _tok_blocks(f(inp["hyb_w_out"])[0])
    wqkv = f(inp["sb_w_qkv"])[0]
    sh["wqk_f"] = _feat_units(wqkv[:, :4096])
    sh["wv_f"] = _tok_blocks(np.ascontiguousarray(wqkv[:, 4096:]))
    sh["wo_f"] = _tok_blocks(f(inp["sb_w_o"])[0])
    for l in range(2):
        sh["wup%d_f" % l] = _feat_units(f(inp["mlp_w_up"])[l])
        sh["wdn%d_f" % l] = _tok_blocks(f(inp["mlp_w_down"])[l])
    return sh


_CACHE = {}


def kernel(**inputs):
    if "nc" not in _CACHE:
        _CACHE["nc"] = build()[0]
    nc = _CACHE["nc"]
    sh = prep_shared(inputs)
    x = np.asarray(inputs["x"], dtype=np.float32)
    in_maps = []
    for c in range(8):
        m = dict(sh)
        m["x"] = np.ascontiguousarray(x[c])
        in_maps.append(m)
    res = run_bass_kernel_spmd(nc, in_maps, core_ids=list(range(8)))
    return np.stack([np.asarray(r["y"], dtype=np.float32) for r in res.results], axis=0)
```

```python
import contextlib
import math
import numpy as np
import concourse.bass as bass
import concourse.mybir as mybir
from concourse.bass_utils import run_bass_kernel_spmd

F32 = mybir.dt.float32
BF16 = mybir.dt.bfloat16
AF = mybir.ActivationFunctionType
ALU = mybir.AluOpType

EPOCH = 30000
S_LEN = 4096
D = 2048
T = 512
NT = S_LEN // T
NKC = D // 128
EPS = 1e-6


class Op:
    __slots__ = ("idx", "eng", "emit", "deps", "dma", "dma_val", "signal", "sig")


class Sched:
    def __init__(self):
        self.ops = []
        self.last_w = {}
        self.readers = {}
        self.dma_cnt = {}
        self.dma_last = {}
        self.last_compute = {}
        self.phase_op = None

    def op(self, eng, emit, reads=(), writes=(), dma=None, barrier=False):
        o = Op()
        o.idx = len(self.ops)
        o.eng = eng
        o.emit = emit
        o.dma = dma
        o.signal = False
        o.sig = None
        deps = set()
        if barrier:
            deps.update(self.last_compute.values())
            deps.update(self.dma_last.values())
        elif self.phase_op is not None:
            deps.add(self.phase_op)
        for r in reads:
            w = self.last_w.get(r)
            if w is not None:
                deps.add(w)
        for r in writes:
            w = self.last_w.get(r)
            if w is not None:
                deps.add(w)
            for rd in self.readers.get(r, {}).values():
                deps.add(rd)
        dd = {}
        for d in deps:
            od = self.ops[d]
            if od.dma is not None:
                dd[d] = 16 * self.dma_cnt[od.dma]
            else:
                if od.eng == "pe" and eng == "pe" and dma is None:
                    continue
                dd[d] = None
        o.deps = dd
        rkey = eng if dma is None else ("dma", o.idx)
        for r in reads:
            self.readers.setdefault(r, {})[rkey] = o.idx
        for r in writes:
            self.last_w[r] = o.idx
            self.readers[r] = {}
        if dma is not None:
            c = self.dma_cnt.get(dma, 0) + 1
            self.dma_cnt[dma] = c
            o.dma_val = 16 * c
            self.dma_last[dma] = o.idx
        else:
            self.last_compute[eng] = o.idx
        if barrier:
            self.phase_op = o.idx
        self.ops.append(o)
        return o

    def barrier(self):
        self.op("dve", lambda e: e.memset(self.bar_t[:], 0.0), barrier=True)

    def finalize(self):
        for o in self.ops:
            for d, v in o.deps.items():
                if v is None:
                    self.ops[d].signal = True
        cnt = {}
        for o in self.ops:
            if o.signal:
                k = cnt.get(o.eng, 0)
                cnt[o.eng] = k + 1
                o.sig = (o.eng, k // EPOCH, k % EPOCH + 1)
        self.n_epochs = {e: (c + EPOCH - 1) // EPOCH for e, c in cnt.items()}
        return cnt

    def emit_all(self, nc, stack):
        cnt = self.finalize()
        sems = {}
        for e, n in self.n_epochs.items():
            for ep in range(n):
                sems[("eng", e, ep)] = stack.enter_context(nc.semaphore(f"s_{e}_{ep}"))
        for k in self.dma_cnt:
            sems[("dma", k)] = stack.enter_context(nc.semaphore(f"d_{len(sems)}"))
        streams = {}
        for o in self.ops:
            streams.setdefault(o.eng, []).append(o)
        ops = self.ops

        def run_stream(engobj, lst):
            waited = {}
            for o in lst:
                need = {}
                for d, v in o.deps.items():
                    od = ops[d]
                    if od.dma is not None:
                        key = ("dma", od.dma)
                        val = v
                    else:
                        key = ("eng", od.sig[0], od.sig[1])
                        val = od.sig[2]
                    if waited.get(key, 0) >= val:
                        continue
                    if need.get(key, 0) < val:
                        need[key] = val
                for key, val in need.items():
                    engobj.wait_ge(sems[key], val)
                    waited[key] = val
                ins = o.emit(engobj)
                if o.dma is not None:
                    ins.then_inc(sems[("dma", o.dma)], 16)
                elif o.signal:
                    ins.then_inc(sems[("eng", o.sig[0], o.sig[1])], 1)

        block = stack.enter_context(nc.Block())
        if "sp" in streams:
            @block.sync
            def _(e):
                run_stream(e, streams["sp"])
        if "pe" in streams:
            @block.tensor
            def _(e):
                run_stream(e, streams["pe"])
        if "act" in streams:
            @block.scalar
            def _(e):
                run_stream(e, streams["act"])
        if "dve" in streams:
            @block.vector
            def _(e):
                run_stream(e, streams["dve"])
        if "pool" in streams:
            @block.gpsimd
            def _(e):
                run_stream(e, streams["pool"])
        return cnt


class Ring:
    def __init__(self, name, n):
        self.name = name
        self.n = n
        self.i = 0

    def next(self):
        s = self.i % self.n
        self.i += 1
        return s, (self.name, s)


WSPEC = {
    "win": (40 * 128, 2048),
    "wout": (4 * 2 * 128, 4096),
    "wup0": (64 * 128, 2048),
    "wdn0": (4 * 8 * 128, 4096),
    "wqk": (32 * 128, 2048),
    "wv": (4 * 2 * 128, 4096),
    "wo": (4 * 2 * 128, 4096),
    "wup1": (64 * 128, 2048),
    "wdn1": (4 * 8 * 128, 4096),
}
NVEC = 24 + 32 + 8 * 4
V_CA, V_CB, V_CBB, V_BA, V_BX, V_LAM = 0, 24, 56, 64, 72, 80
ALL_PHASES = ("p1", "p2", "p3a", "p3b", "p3c", "p4")


def build(phases=ALL_PHASES, dbg=()):
    nc = bass.Bass("TRN2", target_bir_lowering=False)
    S = Sched()

    def dram(name, shape, dt, kind=None):
        if kind is None and name in dbg:
            kind = "ExternalOutput"
        if kind is None:
            return nc.dram_tensor(name, shape, dt).ap()
        return nc.dram_tensor(name, shape, dt, kind=kind).ap()

    x_d = dram("x", [S_LEN, D], F32, "ExternalInput")
    gbc_d = dram("gbc", [8 * 128, D], F32, "ExternalInput")
    vec_d = dram("vec", [128, NVEC], F32, "ExternalInput")
    cst_d = dram("cst", [128, 5 * 128], F32, "ExternalInput")
    wbd_d = dram("wbd", [128, 16 * 128], F32, "ExternalInput")
    wf = {k: dram(k + "_f", list(v), F32, "ExternalInput") for k, v in WSPEC.items()}
    wb = {k: dram(k + "_b", list(v), BF16) for k, v in WSPEC.items()}
    y_d = dram("y", [S_LEN, D], F32, "ExternalOutput")
    xA = dram("xA", [S_LEN, D], F32)
    xB = dram("xB", [S_LEN, D], F32)
    xC = dram("xC", [S_LEN, D], F32)
    qT_s = dram("qT_s", [16 * 128, S_LEN], BF16)
    kT_s = dram("kT_s", [16 * 128, S_LEN], BF16)
    v_s = dram("v_s", [S_LEN, D], BF16)
    aT_s = dram("aT_s", [D, S_LEN], BF16)

    with contextlib.ExitStack() as top:
        _uid = [0]

        def sbt(st, n, sh, dt):
            _uid[0] += 1
            return st.enter_context(nc.sbuf_tensor("sb%d_%s" % (_uid[0], n), sh, dt))
        PB = [top.enter_context(nc.psum_tensor(f"pb{i}", [128, 512], F32)) for i in range(6)]
        TP = [top.enter_context(nc.psum_tensor(f"tp{i}", [128, 1024], BF16)) for i in range(2)]
        bankring = Ring("ps", 6)
        cstf = sbt(top, "cstf", [128, 5 * 128], F32)
        cstb = sbt(top, "cstb", [128, 5 * 128], BF16)
        vec = sbt(top, "vec", [128, NVEC], F32)
        epst = sbt(top, "epst", [128, 1], F32)
        onet = sbt(top, "onet", [128, 1], F32)
        S.bar_t = sbt(top, "bar_t", [128, 1], F32)
        ident = cstb[:, 0:128]
        ntri = cstb[:, 128:256]
        nones = cstb[:, 256:384]
        negm = cstb[:, 384:512]
        zeros = cstb[:, 512:640]

        S.op("sp", lambda e: e.dma_start(out=cstf[:], in_=cst_d), writes=["cstf"], dma="c0")
        S.op("sp", lambda e: e.dma_start(out=vec[:], in_=vec_d), writes=["vec"], dma="c1")
        S.op("dve", lambda e: e.tensor_copy(out=cstb[:], in_=cstf[:]), reads=["cstf"], writes=["cstb"])
        S.op("dve", lambda e: e.memset(epst[:], EPS), writes=["epst"])
        S.op("dve", lambda e: e.memset(onet[:], 1.0), writes=["onet"])
        cast_jobs = []

        def add_cast(k):
            rows, cols = WSPEC[k]
            step = (8 << 20) // (cols * 4)
            for r0 in range(0, rows, step):
                cast_jobs.append((k, r0, min(rows, r0 + step), r0 // step))

        def issue_casts(n):
            for _ in range(n):
                if not cast_jobs:
                    return
                k, r0, r1, ci = cast_jobs.pop(0)
                sem = "cast_%s_%d" % (k, ci) if k in ("win", "wout") else "cast_" + k
                S.op("pool", lambda e, k=k, r0=r0, r1=r1: e.dma_start(out=wb[k][r0:r1, :], in_=wf[k][r0:r1, :]),
                     writes=[("wb", k, ci)], dma=sem)

        for ph, ws in (("p1", ["win", "wout"]), ("p2", ["wup0", "wdn0"]), ("p3a", ["wqk", "wv"]), ("p3c", ["wo"]), ("p4", ["wup1", "wdn1"])):
            if ph in phases:
                for k in ws:
                    add_cast(k)
        issue_casts(7 if "p1" in phases else 4)

        def wres_rows(k, r0, r1):
            step = (8 << 20) // (WSPEC[k][1] * 4)
            return [("wb", k, i) for i in range(r0 // step, (r1 - 1) // step + 1)]

        def prep_tile(st_bufs, xsrc, t, gidx, hname="hT", part="all", ctx=None, tbs=(0, 1, 2, 3)):
            xin, xring, hbf, hring, hT, gbc, small = st_bufs
            if ctx is None:
                ctx = {}
            if part == "L":
                for tb in tbs:
                    xs, xres = xring.next()
                    ctx[("x", tb)] = (xs, xres)
                    r0 = t * T + tb * 128
                    S.op("act", lambda e, xs=xs, r0=r0: e.dma_start(out=xin[:, xs, :], in_=xsrc[r0:r0 + 128, :]),
                         reads=[("xd", id(xsrc), r0 // 128)], writes=[xres], dma="xin%d" % xs)
                return ctx
            if part in ("all", "A"):
                S.op("dve", lambda e: e.memset(small[:, tbs[0]:tbs[-1] + 1], 0.0), writes=[("ssA", i) for i in tbs])
                if len(tbs) <= 2:
                    for tb in tbs:
                        if ("x", tb) in ctx:
                            continue
                        xs, xres = xring.next()
                        ctx[("x", tb)] = (xs, xres)
                        r0 = t * T + tb * 128
                        S.op("act", lambda e, xs=xs, r0=r0: e.dma_start(out=xin[:, xs, :], in_=xsrc[r0:r0 + 128, :]),
                             reads=[("xd", id(xsrc), r0 // 128)], writes=[xres], dma="xin%d" % xs)
            for tb in tbs:
                if part in ("all", "A"):
                    hs, hres = hring.next()
                    ctx[tb] = (hs, hres)
                    r0 = t * T + tb * 128
                    if ("x", tb) in ctx:
                        xs, xres = ctx[("x", tb)]
                    else:
                        xs, xres = xring.next()
                        S.op("act", lambda e, xs=xs, r0=r0: e.dma_start(out=xin[:, xs, :], in_=xsrc[r0:r0 + 128, :]),
                             reads=[("xd", id(xsrc), r0 // 128)], writes=[xres], dma="xin%d" % xs)
                    ss = small[:, tb:tb + 1]
                    sd = small[:, 4 + tb:5 + tb]
                    rs = small[:, 8 + tb:9 + tb]
                    S.op("act", lambda e, xs=xs, hs=hs, ss=ss: e.activation(out=hbf[:, hs, :], in_=xin[:, xs, :], func=AF.Square, accum_out=ss),
                         reads=[xres, ("ssA", tb)], writes=[hres, ("ssA", tb)])
                    S.op("act", lambda e, ss=ss, sd=sd: e.activation(out=sd, in_=ss, func=AF.Sqrt, bias=epst[:], scale=1.0 / D),
                         reads=[("ssA", tb), "epst"], writes=[("sdA", tb)])
                    S.op("dve", lambda e, sd=sd, rs=rs: e.reciprocal(out=rs, in_=sd), reads=[("sdA", tb)], writes=[("rsA", tb)])
                    S.op("dve", lambda e, xs=xs, hs=hs, rs=rs: e.scalar_tensor_tensor(out=hbf[:, hs, :], in0=xin[:, xs, :], scalar=rs, in1=gbc[:],
                                                                                     op0=ALU.mult, op1=ALU.mult),
                         reads=[xres, ("rsA", tb), "gpre"], writes=[hres])
                if part in ("all", "B"):
                    hs, hres = ctx[tb]
                    for half in range(2):
                        for k in range(8):
                            kc = half * 8 + k
                            S.op("pe", lambda e, half=half, k=k, kc=kc, hs=hs: e.transpose(out=TP[half][:, k * 128:(k + 1) * 128],
                                                                                           in_=hbf[:, hs, kc * 128:(kc + 1) * 128], identity=ident),
                                 reads=[hres, "cstb"], writes=[("tp", half)])
                        dst = hT[:, half * 8:(half + 1) * 8, tb * 128:(tb + 1) * 128]
                        src = TP[half][:].rearrange("p (k t) -> p k t", k=8)
                        if half == 0:
                            S.op("act", lambda e, dst=dst, src=src: e.activation(out=dst, in_=src, func=AF.Copy),
                                 reads=[("tp", half)], writes=[(hname, tb)])
                        else:
                            S.op("dve", lambda e, dst=dst, src=src: e.tensor_copy(out=dst, in_=src),
                                 reads=[("tp", half)], writes=[(hname, tb)])
            return ctx

        def gemm_feat(rhs_of, rhs_res, nkc, wname, nunits, G, wbuf, wring, evac):
            for u0 in range(0, nunits, G):
                ws, wr = wring.next()
                S.op("sp", lambda e, ws=ws, u0=u0: e.dma_start(out=wbuf[:, ws, :, :],
                                                               in_=wb[wname][u0 * 128:(u0 + G) * 128, :].rearrange("(g p) l -> p g l", p=128)),
                     reads=wres_rows(wname, u0 * 128, (u0 + G) * 128), writes=[wr], dma="wf%d" % ws)
                for g in range(G):
                    b, br = bankring.next()
                    for kc in range(nkc):
                        S.op("pe", lambda e, b=b, ws=ws, g=g, kc=kc: e.matmul(PB[b][:], lhsT=wbuf[:, ws, g, kc * 128:(kc + 1) * 128],
                                                                               rhs=rhs_of(kc), start=(kc == 0), stop=(kc == nkc - 1)),
                             reads=[wr] + rhs_res, writes=[br])
                    evac(u0 + g, b, br)

        def gemm_tok(lhs_of, lhs_res, nkc, wname, wbuf, wring, evac, ks_order=None):
            nks = nkc // 8
            for nb in range(4):
                banks = [bankring.next() for _ in range(4)]
                korder = list(ks_order) if ks_order is not None else list(range(nks))
                for ki, ks in enumerate(korder):
                    ws, wr = wring.next()
                    blk = nb * nks + ks
                    S.op("sp", lambda e, ws=ws, blk=blk: e.dma_start(out=wbuf[:, ws, :], in_=wb[wname][blk * 128:(blk + 1) * 128, :]),
                         reads=wres_rows(wname, blk * 128, (blk + 1) * 128), writes=[wr], dma="wt%d" % ws)
                    for tb in range(4):
                        b, br = banks[tb]
                        for k in range(8):
                            kc = ks * 8 + k
                            S.op("pe", lambda e, b=b, ws=ws, k=k, kc=kc, tb=tb, ki=ki: e.matmul(PB[b][:], lhsT=lhs_of(kc)[:, tb * 128:(tb + 1) * 128],
                                                                                         rhs=wbuf[:, ws, k * 512:(k + 1) * 512],
                                                                                         start=(ki == 0 and k == 0), stop=(ki == nks - 1 and k == 7)),
                                 reads=[wr] + lhs_res, writes=[br])
                for tb in range(4):
                    b, br = banks[tb]
                    evac(tb, nb, b, br)

        def post_init(xsrc, xdst, t):
            r0 = t * T
            S.op("pool", lambda e: e.dma_start(out=xdst[r0:r0 + T, :], in_=xsrc[r0:r0 + T, :]),
                 reads=[("xd", id(xsrc), r0 // 128 + i) for i in range(4)], writes=[("xd", id(xdst), r0 // 128 + i) for i in range(4)], dma="xcp")

        def post_tile(mix, xin, xring, gbc, small, junk, xsrc, xdst, t, gidx, final=False):
            S.op("dve", lambda e: e.memset(small[:, 12:16], 0.0), writes=[("ssB", i) for i in range(4)])
            for tb in range(4):
                r0 = t * T + tb * 128
                ss = small[:, 12 + tb:13 + tb]
                sd = small[:, 16 + tb:17 + tb]
                rs = small[:, 20 + tb:21 + tb]
                mres = ("mix", tb)
                S.op("act", lambda e, tb=tb, ss=ss: e.activation(out=junk[:], in_=mix[:, tb, :], func=AF.Square, accum_out=ss),
                     reads=[mres, ("ssB", tb)], writes=["junk", ("ssB", tb)])
                S.op("act", lambda e, ss=ss, sd=sd: e.activation(out=sd, in_=ss, func=AF.Sqrt, bias=epst[:], scale=1.0 / D),
                     reads=[("ssB", tb), "epst"], writes=[("sdB", tb)])
                S.op("dve", lambda e, sd=sd, rs=rs: e.reciprocal(out=rs, in_=sd), reads=[("sdB", tb)], writes=[("rsB", tb)])
                S.op("dve", lambda e, tb=tb, rs=rs: e.scalar_tensor_tensor(out=mix[:, tb, :], in0=mix[:, tb, :], scalar=rs, in1=gbc[:],
                                                                           op0=ALU.mult, op1=ALU.mult),
                     reads=[mres, ("rsB", tb), "gpost"], writes=[mres])
                S.op("pool", lambda e, tb=tb, r0=r0: e.dma_start(out=xdst[r0:r0 + 128, :], in_=mix[:, tb, :], accum_op=ALU.add),
                     reads=[mres], writes=[("xd", id(xdst), r0 // 128)], dma="xout%d" % tb)

        def load_gains(gpre, gi_pre, gpost, gi_post):
            if gpre is not None:
                S.op("sp", lambda e: e.dma_start(out=gpre[:], in_=gbc_d[gi_pre * 128:(gi_pre + 1) * 128, :]), writes=["gpre"], dma="gbc")
            if gpost is not None:
                S.op("sp", lambda e: e.dma_start(out=gpost[:], in_=gbc_d[gi_post * 128:(gi_post + 1) * 128, :]), writes=["gpost"], dma="gbc")

        def phase_p1():
            with contextlib.ExitStack() as st:
                xin = sbt(st, "xin", [128, 2, D], F32)
                hbf = sbt(st, "hbf", [128, 4, D], BF16)
                hT = sbt(st, "hT", [128, NKC, T], BF16)
                yT = sbt(st, "yT", [128, NKC, T], BF16)
                mix = sbt(st, "mix", [128, 4, D], F32)
                gbc = sbt(st, "gpre", [128, D], F32)
                gpost = sbt(st, "gpost", [128, D], F32)
                small = sbt(st, "small", [128, 32], F32)
                wfb = sbt(st, "wfb", [128, 2, 3, D], BF16)
                wtb = sbt(st, "wtb", [128, 2, 4096], BF16)
                uev = sbt(st, "uev", [128, 4, T], F32)
                cx = sbt(st, "cx", [128, 1, T + 2], F32)
                ycv = sbt(st, "ycv", [128, 1, T], F32)
                bxh = sbt(st, "bxh", [128, 1, T + 3], F32)
                xr = sbt(st, "xr", [128, 2, T], F32)
                xrb = sbt(st, "xrb", [128, 2, T], BF16)
                tr_ = sbt(st, "tr_", [128, 1, T], F32)
                ti_ = sbt(st, "ti_", [128, 1, T], F32)
                ta_ = sbt(st, "ta_", [128, 1, T], F32)
                tm_ = sbt(st, "tm_", [128, 1, T], F32)
                tb_ = sbt(st, "tb_", [128, 1, T], F32)
                hs_ = sbt(st, "hs_", [128, 1, T], F32)
                gl_ = sbt(st, "gl_", [128, 1, T], F32)
                gu_ = sbt(st, "gu_", [128, 1, T], F32)
                haloA = sbt(st, "haloA", [128, 8, 2], F32)
                haloB = sbt(st, "haloB", [128, 8, 3], F32)
                hstate = sbt(st, "hstate", [128, 8], F32)
                cl = sbt(st, "cl", [128, 8], F32)
                wbdb = sbt(st, "wbdb", [128, 16 * 128], BF16)
                junkp = sbt(st, "junkp", [128, D], BF16)
                xring, hring = Ring("xin", 2), Ring("hbf", 4)
                wfring, wtring, uring = Ring("wfb", 2), Ring("wtb", 2), Ring("uev", 4)
                r2 = {n: Ring(n, 2) for n in ("cx", "ycv", "bxh", "xr", "xrb", "tr", "ti", "ta", "tm", "tb", "hs", "gl", "gu")}
                for n in ("gu", "tm", "tb", "tr", "ti", "gl", "ta", "cx", "ycv", "bxh", "hs"):
                    r2[n] = Ring(n, 1)

                S.op("pool", lambda e: e.dma_start(out=wbdb[:], in_=wbd_d), writes=["wbdb"], dma="c2")
                S.op("dve", lambda e: e.memset(haloA[:], 0.0), writes=["haloA"])
                S.op("dve", lambda e: e.memset(haloB[:], 0.0), writes=["haloB"])
                S.op("dve", lambda e: e.memset(hstate[:], 0.0), writes=["hstate"])
                S.op("act", lambda e: e.activation(out=cl[:], in_=vec[:, V_LAM:V_LAM + 8], func=AF.Exp, scale=-1.0), reads=["vec"], writes=["cl"])
                S.op("act", lambda e: e.activation(out=cl[:], in_=cl[:], func=AF.Ln, bias=onet[:], scale=1.0), reads=["cl", "onet"], writes=["cl"])
                S.op("dve", lambda e: e.tensor_scalar(out=cl[:], in0=cl[:], scalar1=-8.0, scalar2=None, op0=ALU.mult), reads=["cl"], writes=["cl"])

                bufs = (xin, xring, hbf, hring, hT, gbc, small)
                load_gains(gbc, 0, gpost, 1)
                prep_tile(bufs, x_d, 0, 0)
                for t in range(NT):
                    issue_casts(2)
                    post_init(x_d, xA, t)
                    ev = {}
                    bctx = {}

                    def evac_in(u, b, br):
                        c, r = divmod(u, 5)
                        kind = (3, 4, 0, 1, 2)[r]
                        if kind == 4:
                            s, res = r2["bxh"].next()
                            S.op("pool", lambda e, s=s, c=c: e.tensor_copy(out=bxh[:, s, 0:3], in_=haloB[:, c, :]), reads=["haloB"], writes=[res])
                            S.op("act", lambda e, s=s, b=b: e.activation(out=bxh[:, s, 3:T + 3], in_=PB[b][:], func=AF.Copy), reads=[br], writes=[res])
                            S.op("pool", lambda e, s=s, c=c: e.tensor_copy(out=haloB[:, c, :], in_=bxh[:, s, T:T + 3]), reads=[res], writes=["haloB"])
                            ev[kind] = (bxh[:, s, :], res)
                        else:
                            s, res = uring.next()
                            S.op("act", lambda e, s=s, b=b: e.activation(out=uev[:, s, :], in_=PB[b][:], func=AF.Copy), reads=[br], writes=[res])
                            ev[kind] = (uev[:, s, :], res)
                        if kind == 2:
                            mixer_a(c)
                            mixer_b2(c)
                        if kind == 4:
                            mixer_b1(c)

                    def mixer_a(c):
                        (bg, bgr), (cg, cgr), (ax, axr) = ev[0], ev[1], ev[2]
                        s, cres = r2["cx"].next()
                        ys, yres = r2["ycv"].next()
                        w = lambda k: vec[:, V_CA + c * 3 + k:V_CA + c * 3 + k + 1]
                        S.op("pool", lambda e: e.tensor_copy(out=cx[:, s, 0:2], in_=haloA[:, c, :]), reads=["haloA"], writes=[cres])
                        S.op("dve", lambda e: e.tensor_tensor(out=cx[:, s, 2:T + 2], in0=cg, in1=ax, op=ALU.mult), reads=[cgr, axr], writes=[cres])
                        S.op("pool", lambda e: e.tensor_copy(out=haloA[:, c, :], in_=cx[:, s, T:T + 2]), reads=[cres], writes=["haloA"])
                        S.op("dve", lambda e: e.tensor_scalar(out=ycv[:, ys, :], in0=cx[:, s, 2:T + 2], scalar1=w(2), scalar2=None, op0=ALU.mult),
                             reads=[cres, "vec"], writes=[yres])
                        S.op("dve", lambda e: e.scalar_tensor_tensor(out=ycv[:, ys, :], in0=cx[:, s, 1:T + 1], scalar=w(1), in1=ycv[:, ys, :],
                                                                     op0=ALU.mult, op1=ALU.add), reads=[cres, yres], writes=[yres])
                        S.op("dve", lambda e: e.scalar_tensor_tensor(out=ycv[:, ys, :], in0=cx[:, s, 0:T], scalar=w(0), in1=ycv[:, ys, :],
                                                                     op0=ALU.mult, op1=ALU.add), reads=[cres, yres], writes=[yres])
                        S.op("pool", lambda e: e.tensor_tensor(out=yT[:, c, :], in0=bg, in1=ycv[:, ys, :], op=ALU.mult),
                             reads=[bgr, yres], writes=[("yT", c)])

                    def mixer_b1(c):
                        (g, gr), (bx, bxr) = ev[3], ev[4]
                        nx = {n: r2[n].next() for n in ("xr", "xrb")}
                        XR, XRB = xr[:, nx["xr"][0], :], xrb[:, nx["xrb"][0], :]
                        rr = {n: nx[n][1] for n in nx}
                        w = lambda k: vec[:, V_CB + c * 4 + k:V_CB + c * 4 + k + 1]
                        col = lambda base: vec[:, base + c:base + c + 1]
                        S.op("dve", lambda e: e.tensor_scalar(out=XR, in0=bx[:, 3:T + 3], scalar1=w(3), scalar2=col(V_CBB), op0=ALU.mult, op1=ALU.add),
                             reads=[bxr, "vec"], writes=[rr["xr"]])
                        for k in (2, 1, 0):
                            S.op("dve", lambda e, k=k: e.scalar_tensor_tensor(out=XR, in0=bx[:, k:T + k], scalar=w(k), in1=XR, op0=ALU.mult, op1=ALU.add),
                                 reads=[bxr, rr["xr"]], writes=[rr["xr"]])
                        S.op("pool", lambda e: e.tensor_copy(out=XRB, in_=XR), reads=[rr["xr"]], writes=[rr["xrb"]])
                        bctx[c] = (g, gr, nx)

                    def mixer_b2(c):
                        g, gr, nx0 = bctx[c]
                        nx = {n: r2[n].next() for n in ("tr", "ti", "ta", "tm", "tb", "hs", "gl", "gu")}
                        nx.update(nx0)
                        sl = lambda buf, n: buf[:, nx[n][0], :]
                        w = lambda k: vec[:, V_CB + c * 4 + k:V_CB + c * 4 + k + 1]
                        col = lambda base: vec[:, base + c:base + c + 1]
                        XR, XRB, R, I, A, M, Bv, H, GL, GU = (sl(xr, "xr"), sl(xrb, "xrb"), sl(tr_, "tr"), sl(ti_, "ti"), sl(ta_, "ta"),
                                                              sl(tm_, "tm"), sl(tb_, "tb"), sl(hs_, "hs"), sl(gl_, "gl"), sl(gu_, "gu"))
                        rr = {n: nx[n][1] for n in nx}
                        b1, br1 = bankring.next()
                        b2, br2 = bankring.next()
                        S.op("pe", lambda e: e.matmul(PB[b1][:], lhsT=wbdb[:, c * 128:(c + 1) * 128], rhs=XRB, start=True, stop=True),
                             reads=[rr["xrb"], "wbdb"], writes=[br1])
                        S.op("pe", lambda e: e.matmul(PB[b2][:], lhsT=wbdb[:, (8 + c) * 128:(9 + c) * 128], rhs=XRB, start=True, stop=True),
                             reads=[rr["xrb"], "wbdb"], writes=[br2])
                        S.op("act", lambda e: e.activation(out=R, in_=PB[b1][:], func=AF.Sigmoid, bias=col(V_BA), scale=1.0), reads=[br1, "vec"], writes=[rr["tr"]])
                        S.op("act", lambda e: e.activation(out=I, in_=PB[b2][:], func=AF.Sigmoid, bias=col(V_BX), scale=1.0), reads=[br2, "vec"], writes=[rr["ti"]])
                        S.op("pool", lambda e: e.tensor_tensor(out=GU, in0=g, in1=g, op=ALU.mult), reads=[gr], writes=[rr["gu"]])
                        S.op("pool", lambda e: e.tensor_scalar(out=GU, in0=GU, scalar1=0.044715, scalar2=1.0, op0=ALU.mult, op1=ALU.add),
                             reads=[rr["gu"]], writes=[rr["gu"]])
                        S.op("pool", lambda e: e.tensor_tensor(out=GU, in0=GU, in1=g, op=ALU.mult), reads=[rr["gu"], gr], writes=[rr["gu"]])
                        S.op("act", lambda e: e.activation(out=GL, in_=GU, func=AF.Sigmoid, scale=1.5957691216), reads=[rr["gu"]], writes=[rr["gl"]])
                        S.op("pool", lambda e: e.tensor_tensor(out=GL, in0=GL, in1=g, op=ALU.mult), reads=[rr["gl"], gr], writes=[rr["gl"]])
                        S.op("act", lambda e: e.activation(out=A, in_=R, func=AF.Exp, scale=cl[:, c:c + 1]), reads=[rr["tr"], "cl"], writes=[rr["ta"]])
                        S.op("pool", lambda e: e.tensor_tensor(out=M, in0=A, in1=A, op=ALU.mult), reads=[rr["ta"]], writes=[rr["tm"]])
                        S.op("act", lambda e: e.activation(out=M, in_=M, func=AF.Sqrt, bias=onet[:], scale=-1.0), reads=[rr["tm"], "onet"], writes=[rr["tm"]])
                        S.op("dve", lambda e: e.tensor_tensor(out=Bv, in0=I, in1=XR, op=ALU.mult), reads=[rr["ti"], rr["xr"]], writes=[rr["tb"]])
                        S.op("dve", lambda e: e.tensor_tensor(out=Bv, in0=Bv, in1=M, op=ALU.mult), reads=[rr["tb"], rr["tm"]], writes=[rr["tb"]])
                        S.op("dve", lambda e: e.tensor_tensor_scan(out=H, data0=A, data1=Bv, initial=hstate[:, c:c + 1], op0=ALU.mult, op1=ALU.add),
                             reads=[rr["ta"], rr["tb"], "hstate"], writes=[rr["hs"]])
                        S.op("dve", lambda e: e.tensor_copy(out=hstate[:, c:c + 1], in_=H[:, T - 1:T]), reads=[rr["hs"]], writes=["hstate"])
                        S.op("dve", lambda e: e.tensor_tensor(out=yT[:, 8 + c, :], in0=H, in1=GL, op=ALU.mult), reads=[rr["hs"], rr["gl"]], writes=[("yT", 8 + c)])

                    pctx = {}

                    def hook0(t=t, pctx=pctx):
                        if t + 1 < NT:
                            prep_tile(bufs, x_d, t + 1, 0, part="L", ctx=pctx, tbs=(0, 1))

                    def hook1(t=t, pctx=pctx):
                        if t > 0:
                            post_tile(mix, xin, xring, gpost, small, junkp, x_d, xA, t - 1, 1)
                        if t + 1 < NT:
                            prep_tile(bufs, x_d, t + 1, 0, part="A", ctx=pctx, tbs=(0, 1))
                            prep_tile(bufs, x_d, t + 1, 0, part="L", ctx=pctx, tbs=(2, 3))

                    def hook2(t=t, pctx=pctx):
                        if t + 1 < NT:
                            prep_tile(bufs, x_d, t + 1, 0, part="A", ctx=pctx, tbs=(2, 3))
                    _gemm_feat_groups(hT, "win", [2, 3] * 8, wfb, wfring, evac_in, mid_hook=[(1, hook0), (5, hook1), (11, hook2)])
                    if t + 1 < NT:
                        prep_tile(bufs, x_d, t + 1, 0, part="B", ctx=pctx)

                    def evac_out(tb, nb, b, br):
                        S.op("act", lambda e: e.activation(out=mix[:, tb, nb * 512:(nb + 1) * 512], in_=PB[b][:], func=AF.Copy),
                             reads=[br], writes=[("mix", tb)])
                    gemm_tok(lambda kc: yT[:, kc, :], [("yT", i) for i in range(16)], NKC, "wout", wtb, wtring, evac_out)
                    if t == NT - 1:
                        post_tile(mix, xin, xring, gpost, small, junkp, x_d, xA, t, 1)

        def _gemm_feat_groups(hT, wname, groups, wbuf, wring, evac, nkc=NKC, hname="hT", mid_hook=None):
            u0 = 0
            for gi, G in enumerate(groups):
                if mid_hook is not None:
                    for hk in (mid_hook if isinstance(mid_hook, list) else [mid_hook]):
                        if gi == hk[0]:
                            hk[1]()
                ws, wr = wring.next()
                S.op("sp", lambda e, ws=ws, u0=u0, G=G: e.dma_start(out=wbuf[:, ws, 0:G, :],
                                                                    in_=wb[wname][u0 * 128:(u0 + G) * 128, :].rearrange("(g p) l -> p g l", p=128)),
                     reads=wres_rows(wname, u0 * 128, (u0 + G) * 128), writes=[wr], dma="wf%d" % ws)
                for g in range(G):
                    b, br = bankring.next()
                    for kc in range(nkc):
                        S.op("pe", lambda e, b=b, ws=ws, g=g, kc=kc: e.matmul(PB[b][:], lhsT=wbuf[:, ws, g, kc * 128:(kc + 1) * 128],
                                                                               rhs=hT[:, kc, :], start=(kc == 0), stop=(kc == nkc - 1)),
                             reads=[wr] + [(hname, i) for i in range(4)], writes=[br])
                    evac(u0 + g, b, br)
                u0 += G

        def phase_mlp(layer, xsrc, xdst):
            with contextlib.ExitStack() as st:
                xin = sbt(st, "xin", [128, 2, D], F32)
                hbf = sbt(st, "hbf", [128, 4, D], BF16)
                hT = sbt(st, "hT", [128, NKC, T], BF16)
                uT = sbt(st, "uT", [128, 64, T], BF16)
                mix = sbt(st, "mix", [128, 4, D], F32)
                gbc = sbt(st, "gpre", [128, D], F32)
                gpost = sbt(st, "gpost", [128, D], F32)
                small = sbt(st, "small", [128, 32], F32)
                wfb = sbt(st, "wfb", [128, 2, 2, D], BF16)
                wtb = sbt(st, "wtb", [128, 2, 4096], BF16)
                rl = sbt(st, "rl", [128, 3, T], F32)
                junkp = sbt(st, "junkp", [128, D], BF16)
                xring, hring = Ring("xin", 2), Ring("hbf", 4)
                wfring, wtring, rring = Ring("wfb", 2), Ring("wtb", 2), Ring("rl", 3)
                bufs = (xin, xring, hbf, hring, hT, gbc, small)
                load_gains(gbc, layer * 4 + 2, gpost, layer * 4 + 3)
                prep_tile(bufs, xsrc, 0, layer * 4 + 2)
                for t in range(NT):
                    issue_casts(1 if layer == 0 else 100)
                    post_init(xsrc, xdst, t)

                    def evac_up(u, b, br):
                        s, res = rring.next()
                        S.op("act", lambda e: e.activation(out=rl[:, s, :], in_=PB[b][:], func=AF.Relu), reads=[br], writes=[res])
                        eng = "pool" if u % 2 == 0 else "dve"
                        S.op(eng, lambda e: e.tensor_tensor(out=uT[:, u, :], in0=rl[:, s, :], in1=rl[:, s, :], op=ALU.mult),
                             reads=[res], writes=[("uT", u)])
                    pctx = {}
                    def mh1(t=t, pctx=pctx):
                        prep_tile(bufs, xsrc, t + 1, layer * 4 + 2, part="A", ctx=pctx, tbs=(0, 1))
                        prep_tile(bufs, xsrc, t + 1, layer * 4 + 2, part="L", ctx=pctx, tbs=(2, 3))
                    hook = [(2, lambda t=t, pctx=pctx: prep_tile(bufs, xsrc, t + 1, layer * 4 + 2, part="L", ctx=pctx, tbs=(0, 1))),
                            (10, mh1),
                            (20, lambda t=t, pctx=pctx: prep_tile(bufs, xsrc, t + 1, layer * 4 + 2, part="A", ctx=pctx, tbs=(2, 3)))] if t + 1 < NT else None
                    _gemm_feat_groups(hT, "wup%d" % layer, [2] * 32, wfb, wfring, evac_up, mid_hook=hook)
                    if t + 1 < NT:
                        prep_tile(bufs, xsrc, t + 1, layer * 4 + 2, part="B", ctx=pctx)

                    def evac_dn(tb, nb, b, br):
                        S.op("act", lambda e: e.activation(out=mix[:, tb, nb * 512:(nb + 1) * 512], in_=PB[b][:], func=AF.Copy),
                             reads=[br], writes=[("mix", tb)])
                    gemm_tok(lambda kc: uT[:, kc, :], [("uT", i) for i in range(64)], 64, "wdn%d" % layer, wtb, wtring, evac_dn)
                    post_tile(mix, xin, xring, gpost, small, junkp, xsrc, xdst, t, layer * 4 + 3)

        def phase_p3a():
            with contextlib.ExitStack() as st:
                xin = sbt(st, "xin", [128, 2, D], F32)
                hbf = sbt(st, "hbf", [128, 4, D], BF16)
                hT2 = sbt(st, "hT2", [128, 2, NKC, T], BF16)
                gbc = sbt(st, "gpre", [128, D], F32)
                small = sbt(st, "small", [128, 32], F32)
                wfb = sbt(st, "wfb", [128, 2, 2, D], BF16)
                wtb = sbt(st, "wtb", [128, 3, 4096], BF16)
                qst = sbt(st, "qst", [128, 4, T], BF16)
                vst = sbt(st, "vst", [128, 4, D], BF16)
                xring, hring = Ring("xin", 2), Ring("hbf", 4)
                wfring, wtring, qring = Ring("wfb", 2), Ring("wtb", 3), Ring("qst", 4)
                qscale = 1.0 / math.sqrt(128.0)
                load_gains(gbc, 4, None, 0)
                mkbufs = lambda t: (xin, xring, hbf, hring, hT2[:, t % 2, :, :], gbc, small)
                prep_tile(mkbufs(0), xB, 0, 4, hname="hT0")
                for t in range(NT):
                    issue_casts(1)
                    hT = hT2[:, t % 2, :, :]
                    hname = "hT%d" % (t % 2)

                    def evac_qk(u, b, br):
                        s, res = qring.next()
                        if u < 16:
                            S.op("act", lambda e: e.activation(out=qst[:, s, :], in_=PB[b][:], func=AF.Copy, scale=qscale), reads=[br], writes=[res])
                            dst = qT_s[u * 128:(u + 1) * 128, t * T:(t + 1) * T]
                            dres = ("qT", u, t)
                        else:
                            S.op("dve", lambda e: e.tensor_copy(out=qst[:, s, :], in_=PB[b][:]), reads=[br], writes=[res])
                            dst = kT_s[(u - 16) * 128:(u - 15) * 128, t * T:(t + 1) * T]
                            dres = ("kT", u - 16, t)
                        S.op("pool", lambda e: e.dma_start(out=dst, in_=qst[:, s, :]), reads=[res], writes=[dres], dma="qst%d" % s)
                    pctx = {}
                    def qh1(t=t, pctx=pctx):
                        prep_tile(mkbufs(t + 1), xB, t + 1, 4, hname="hT%d" % ((t + 1) % 2), part="A", ctx=pctx, tbs=(0, 1))
                        prep_tile(mkbufs(t + 1), xB, t + 1, 4, hname="hT%d" % ((t + 1) % 2), part="L", ctx=pctx, tbs=(2, 3))
                    hook = [(1, lambda t=t, pctx=pctx: prep_tile(mkbufs(t + 1), xB, t + 1, 4, hname="hT%d" % ((t + 1) % 2), part="L", ctx=pctx, tbs=(0, 1))),
                            (5, qh1),
                            (11, lambda t=t, pctx=pctx: prep_tile(mkbufs(t + 1), xB, t + 1, 4, hname="hT%d" % ((t + 1) % 2), part="A", ctx=pctx, tbs=(2, 3)))] if t + 1 < NT else None
                    _gemm_feat_groups(hT, "wqk", [2] * 16, wfb, wfring, evac_qk, hname=hname, mid_hook=hook)
                    if t + 1 < NT:
                        prep_tile(mkbufs(t + 1), xB, t + 1, 4, hname="hT%d" % ((t + 1) % 2), part="B", ctx=pctx)

                    def evac_v(tb, nb, b, br):
                        S.op("act", lambda e: e.activation(out=vst[:, tb, nb * 512:(nb + 1) * 512], in_=PB[b][:], func=AF.Copy),
                             reads=[br], writes=[("vst", tb)])
                        if nb == 3:
                            r0 = t * T + tb * 128
                            S.op("pool", lambda e: e.dma_start(out=v_s[r0:r0 + 128, :], in_=vst[:, tb, :]),
                                 reads=[("vst", tb)], writes=[("v", r0 // 128)], dma="vst%d" % tb)
                    gemm_tok(lambda kc, hT=hT: hT[:, kc, :], [(hname, i) for i in range(4)], NKC, "wv", wtb, wtring, evac_v)

        def phase_p3b():
            with contextlib.ExitStack() as st:
                qT = sbt(st, "qT", [128, 2, S_LEN], BF16)
                kT = sbt(st, "kT", [128, 2, S_LEN], BF16)
                vv = sbt(st, "vv", [128, 2, 32, 128], BF16)
                et = sbt(st, "et", [128, 2, T], F32)
                spt = sbt(st, "spt", [128, 4, T], BF16)
                acc = sbt(st, "acc", [128, 6, T], BF16)
                wT = sbt(st, "wT", [128, 4, T], BF16)
                ost = sbt(st, "ost", [128, 2, T], BF16)
                hring = Ring("hd", 2)
                ering, sring, wring_, oring = Ring("et", 2), Ring("spt", 4), Ring("wT", 4), Ring("ost", 2)
                aring, bring = Ring("psA", 2), Ring("psB", 2)
                tiles = []
                sbi = 0
                for h in range(16):
                    hsl, hres = hring.next()
                    for I in range(NT):
                        ob = 4 + (sbi % 2)
                        aset = 3 * (sbi % 2)
                        sbi += 1
                        jtop = 4 * I + 3
                        for pos, j in enumerate(range(jtop, -1, -1)):
                            c0 = max(0, j - 4 * I)
                            tiles.append(dict(h=h, hsl=hsl, hres=hres, I=I, j=j, lo=c0 * 128, diag=(j >= 4 * I), first=(j == jtop), last=(j == 0),
                                              ob=ob, a_old=aset + pos % 3, a_new=aset + (pos + 1) % 3, aset=aset, newhead=(I == 0 and j == jtop)))

                head_first = {tl_["h"]: tl_ for tl_ in tiles if tl_["newhead"]}

                def load_head(tl):
                    h, hsl, hres = tl["h"], tl["hsl"], tl["hres"]
                    if h < 8:
                        issue_casts(1)
                    S.op("sp", lambda e: e.dma_start(out=qT[:, hsl, :], in_=qT_s[h * 128:(h + 1) * 128, :]),
                         reads=[("qT", h, t) for t in range(NT)], writes=[hres], dma="hq%d" % hsl)
                    S.op("sp", lambda e: e.dma_start(out=kT[:, hsl, :], in_=kT_s[h * 128:(h + 1) * 128, :]),
                         reads=[("kT", h, t) for t in range(NT)], writes=[hres], dma="hq%d" % hsl)
                    S.op("sp", lambda e: e.dma_start(out=vv[:, hsl, :, :], in_=v_s[:, h * 128:(h + 1) * 128].rearrange("(j p) d -> p j d", p=128)),
                         reads=[("v", i) for i in range(32)], writes=[hres], dma="hq%d" % hsl)

                def stage1(tl):
                    hsl, hres, I, j, lo, diag = tl["hsl"], tl["hres"], tl["I"], tl["j"], tl["lo"], tl["diag"]
                    ob, obr = tl["ob"], ("ps", tl["ob"])
                    cs = slice(lo, T)
                    qs = slice(I * T + lo, (I + 1) * T)
                    ks = slice(j * 128, (j + 1) * 128)
                    if tl["newhead"] and tl["h"] == 0:
                        load_head(tl)
                    if tl["I"] == 0 and tl["j"] == 1 and tl["h"] + 1 < 16:
                        load_head(head_first[tl["h"] + 1])
                    if tl["first"]:
                        S.op("pe", lambda e: e.matmul(PB[ob][:], lhsT=zeros, rhs=qT[:, hsl, 0:T], start=True, stop=False),
                             reads=["cstb", hres], writes=[obr])
                        for k in range(3):
                            ab = tl["aset"] + k
                            S.op("pool", lambda e, ab=ab: e.memset(acc[:, ab, :], 0.0), writes=[("acc", ab)])
                    ba, bar = aring.next()
                    es, eres = ering.next()
                    ss_, sres = sring.next()
                    tl["ss"], tl["sres"] = ss_, sres
                    S.op("pe", lambda e: e.matmul(PB[ba][:, cs], lhsT=kT[:, hsl, ks], rhs=qT[:, hsl, qs], start=True, stop=not diag),
                         reads=[hres], writes=[("ps", ba)])
                    if diag:
                        S.op("pe", lambda e: e.matmul(PB[ba][:, lo:lo + 128], lhsT=ident, rhs=negm, start=False, stop=True),
                             reads=["cstb"], writes=[("ps", ba)])
                    S.op("act", lambda e: e.activation(out=et[:, es, cs], in_=PB[ba][:, cs], func=AF.Exp), reads=[("ps", ba)], writes=[eres])
                    S.op("act", lambda e: e.activation(out=spt[:, ss_, cs], in_=et[:, es, cs], func=AF.Ln, bias=onet[:], scale=1.0),
                         reads=[eres, "onet"], writes=[sres])
                    if not tl["last"]:
                        a_old, a_new = tl["a_old"], tl["a_new"]
                        S.op("dve", lambda e: e.tensor_tensor(out=acc[:, a_new, cs], in0=acc[:, a_old, cs], in1=spt[:, ss_, cs], op=ALU.add),
                             reads=[("acc", a_old), sres], writes=[("acc", a_new)])

                def stage2(tl):
                    hsl, hres, I, j, lo, diag = tl["hsl"], tl["hres"], tl["I"], tl["j"], tl["lo"], tl["diag"]
                    cs = slice(lo, T)
                    qs = slice(I * T + lo, (I + 1) * T)
                    ks = slice(j * 128, (j + 1) * 128)
                    bb, _ = bring.next()
                    bb += 2
                    bbr = ("ps", bb)
                    ws_, wres_ = wring_.next()
                    tl["ws"], tl["wres"] = ws_, wres_
                    ss_, sres, a_old = tl["ss"], tl["sres"], tl["a_old"]
                    S.op("pe", lambda e: e.matmul(PB[bb][:, cs], lhsT=kT[:, hsl, ks], rhs=qT[:, hsl, qs], start=True, stop=False),
                         reads=[hres], writes=[bbr])
                    if diag:
                        S.op("pe", lambda e: e.matmul(PB[bb][:, lo:lo + 128], lhsT=ident, rhs=negm, start=False, stop=False),
                             reads=["cstb"], writes=[bbr])
                    if not tl["first"]:
                        S.op("pe", lambda e: e.matmul(PB[bb][:, cs], lhsT=nones, rhs=acc[:, a_old, cs], start=False, stop=False),
                             reads=["cstb", ("acc", a_old)], writes=[bbr])
                    S.op("pe", lambda e: e.matmul(PB[bb][:, cs], lhsT=ntri, rhs=spt[:, ss_, cs], start=False, stop=True),
                         reads=["cstb", sres], writes=[bbr])
                    S.op("act", lambda e: e.activation(out=wT[:, ws_, cs], in_=PB[bb][:, cs], func=AF.Exp), reads=[bbr], writes=[wres_])

                def stage3(tl):
                    hsl, hres, I, j, lo, h = tl["hsl"], tl["hres"], tl["I"], tl["j"], tl["lo"], tl["h"]
                    ob, obr = tl["ob"], ("ps", tl["ob"])
                    cs = slice(lo, T)
                    ws_, wres_ = tl["ws"], tl["wres"]
                    S.op("pe", lambda e: e.matmul(PB[ob][:, cs], lhsT=vv[:, hsl, j, :], rhs=wT[:, ws_, cs], start=False, stop=tl["last"]),
                         reads=[hres, wres_], writes=[obr])
                    if tl["last"]:
                        os_, ores = oring.next()
                        S.op("dve", lambda e: e.tensor_copy(out=ost[:, os_, :], in_=PB[ob][:]), reads=[obr], writes=[ores])
                        S.op("pool", lambda e: e.dma_start(out=aT_s[h * 128:(h + 1) * 128, I * T:(I + 1) * T], in_=ost[:, os_, :]),
                             reads=[ores], writes=[("aT", h, I)], dma="ost%d" % os_)

                N = len(tiles)
                for n in range(N + 2):
                    if n < N:
                        stage1(tiles[n])
                    if 0 <= n - 1 < N:
                        stage2(tiles[n - 1])
                    if 0 <= n - 2 < N:
                        stage3(tiles[n - 2])

        def phase_p3c():
            with contextlib.ExitStack() as st:
                xin = sbt(st, "xin", [128, 2, D], F32)
                aT = sbt(st, "aT", [128, 2, NKC, T], BF16)
                mix = sbt(st, "mix", [128, 4, D], F32)
                gbc = sbt(st, "gpost", [128, D], F32)
                junk = sbt(st, "junk", [128, D], BF16)
                small = sbt(st, "small", [128, 32], F32)
                wtb = sbt(st, "wtb", [128, 3, 4096], BF16)
                xring, wtring, aring = Ring("xin", 2), Ring("wtb", 3), Ring("aT", 2)
                issue_casts(100)
                load_gains(None, 0, gbc, 5)
                for t in range(NT):
                    post_init(xB, xC, t)
                    as_, ares = aring.next()
                    S.op("sp", lambda e, as_=as_, t=t: e.dma_start(out=aT[:, as_, :, :],
                                                                   in_=aT_s[:, t * T:(t + 1) * T].rearrange("(kc p) t -> p kc t", p=128)),
                         reads=[("aT", h, t) for h in range(16)], writes=[ares], dma="aT%d" % as_)

                    def evac_o(tb, nb, b, br):
                        S.op("act", lambda e: e.activation(out=mix[:, tb, nb * 512:(nb + 1) * 512], in_=PB[b][:], func=AF.Copy),
                             reads=[br], writes=[("mix", tb)])
                    gemm_tok(lambda kc, as_=as_: aT[:, as_, kc, :], [ares], NKC, "wo", wtb, wtring, evac_o)
                    post_tile(mix, xin, xring, gbc, small, junk, xB, xC, t, 5)

        if "p1" in phases:
            phase_p1()
            S.barrier()
        if "p2" in phases:
            phase_mlp(0, xA, xB)
            S.barrier()
        if "p3a" in phases:
            phase_p3a()
            S.barrier()
        if "p3b" in phases:
            phase_p3b()
            S.barrier()
        if "p3c" in phases:
            phase_p3c()
            S.barrier()
        if "p4" in phases:
            phase_mlp(1, xC, y_d)
        S.op("pool", lambda e: e.nop(), barrier=True)
        cnt = S.emit_all(nc, top)
    return nc, S, cnt


def _feat_units(W, order=None):
    K, N = W.shape
    nu = N // 128
    Wr = W.reshape(K // 128, 128, nu, 128)
    if order is not None:
        Wr = Wr[:, :, order, :]
    return np.ascontiguousarray(Wr.transpose(2, 1, 0, 3)).reshape(nu * 128, (K // 128) * 128)


def _tok_blocks(W):
    K, N = W.shape
    nks = K // 1024
    Wr = W.reshape(nks, 8, 128, N // 512, 512)
    return np.ascontiguousarray(Wr.transpose(3, 0, 2, 1, 4)).reshape((N // 512) * nks * 128, 8 * 512)


def _chunks(v):
    return np.ascontiguousarray(v.reshape(8, 128).T)


def prep_shared(inp):
    f = lambda a: np.ascontiguousarray(np.asarray(a, dtype=np.float32))
    g = f(inp["norm_gains"]).reshape(8, 1, D)
    sh = {}
    sh["gbc"] = np.ascontiguousarray(np.broadcast_to(g, (8, 128, D))).reshape(8 * 128, D)
    ca = f(inp["hyb_conv_a"])[0]
    cb = f(inp["hyb_conv_b"])[0]
    vec = np.zeros((128, NVEC), np.float32)
    vec[:, V_CA:V_CA + 24] = np.stack([_chunks(ca[k]) for k in range(3)], axis=2).reshape(128, 24)
    vec[:, V_CB:V_CB + 32] = np.stack([_chunks(cb[k]) for k in range(4)], axis=2).reshape(128, 32)
    vec[:, V_CBB:V_CBB + 8] = _chunks(f(inp["hyb_conv_b_bias"])[0])
    vec[:, V_BA:V_BA + 8] = _chunks(f(inp["hyb_rg_b_a"])[0])
    vec[:, V_BX:V_BX + 8] = _chunks(f(inp["hyb_rg_b_x"])[0])
    vec[:, V_LAM:V_LAM + 8] = _chunks(f(inp["hyb_rg_lambda"])[0])
    sh["vec"] = vec
    idx = np.arange(128)
    cst = np.zeros((128, 5 * 128), np.float32)
    cst[:, 0:128] = np.eye(128, dtype=np.float32)
    cst[:, 128:256] = -(idx[:, None] >= idx[None, :]).astype(np.float32)
    cst[:, 256:384] = -1.0
    cst[:, 384:512] = np.where(idx[:, None] >= idx[None, :], -30000.0, 0.0)
    sh["cst"] = cst
    wbd = np.zeros((128, 16, 128), np.float32)
    wa = f(inp["hyb_rg_w_a"])[0]
    wx = f(inp["hyb_rg_w_x"])[0]
    for c in range(8):
        for hh in range(2):
            wbd[hh * 64:(hh + 1) * 64, c, hh * 64:(hh + 1) * 64] = wa[2 * c + hh]
            wbd[hh * 64:(hh + 1) * 64, 8 + c, hh * 64:(hh + 1) * 64] = wx[2 * c + hh]
    sh["wbd"] = wbd.reshape(128, 16 * 128)
    order = []
    for c in range(8):
        order += [24 + c, 32 + c, c, 8 + c, 16 + c]
    sh["win_f"] = _feat_units(f(inp["hyb_w_in"])[0], order)
    sh["wout_f"] = _tok_blocks(f(inp["hyb_w_out"])[0])
    wqkv = f(inp["sb_w_qkv"])[0]
    sh["wqk_f"] = _feat_units(wqkv[:, :4096])
    sh["wv_f"] = _tok_blocks(np.ascontiguousarray(wqkv[:, 4096:]))
    sh["wo_f"] = _tok_blocks(f(inp["sb_w_o"])[0])
    for l in range(2):
        sh["wup%d_f" % l] = _feat_units(f(inp["mlp_w_up"])[l])
        sh["wdn%d_f" % l] = _tok_blocks(f(inp["mlp_w_down"])[l])
    return sh


_CACHE = {}


def kernel(**inputs):
    if "nc" not in _CACHE:
        _CACHE["nc"] = build()[0]
    nc = _CACHE["nc"]
    sh = prep_shared(inputs)
    x = np.asarray(inputs["x"], dtype=np.float32)
    in_maps = []
    for c in range(8):
        m = dict(sh)
        m["x"] = np.ascontiguousarray(x[c])
        in_maps.append(m)
    res = run_bass_kernel_spmd(nc, in_maps, core_ids=list(range(8)))
    return np.stack([np.asarray(r["y"], dtype=np.float32) for r in res.results], axis=0)
```
